# Optimizing a Trainium2 kernel written in Bass

```python
import jax, jax.numpy as jnp
from jax import lax
import numpy as np

D_MODEL = 1024
BATCH = 8
SEQ = 2048
DEPTH = 1
DEC_BATCH = 32
DEC_SEQ = 32
PAST_LEN = 2048

CHUNK = 64
PAST_CHUNKS = 8
BAND = PAST_CHUNKS * CHUNK
W_LRU = D_MODEL // 2
LRU_BLOCKS = 8
LRU_BLOCK = W_LRU // LRU_BLOCKS
CONV_W = 4
RG_C = 8.0
HEAD_DIM = 64
W_ATT = D_MODEL // 2
N_HEADS = W_ATT // HEAD_DIM
MAX_REL = 256
PLE_DIM = 256
EPS = 1e-6
NEG = -1e30
IN_WIDTH = 2 * W_LRU + 4 * W_ATT
SPLITS = [W_LRU, 2 * W_LRU, 2 * W_LRU + W_ATT, 2 * W_LRU + 2 * W_ATT, 2 * W_LRU + 3 * W_ATT]

kernel_name = 'hybrid_rglru_chunkband_stream_step'


def rms_norm(x, g):
    xf = x.astype(jnp.float32)
    y = xf * lax.rsqrt(jnp.mean(xf * xf, axis=-1, keepdims=True) + EPS)
    return (y * g.astype(jnp.float32)).astype(x.dtype)


def causal_conv(u, buf, w, b):
    T = u.shape[1]
    full = jnp.concatenate([buf.astype(u.dtype), u], axis=1)
    out = b + sum(full[:, k:k + T] * w[k] for k in range(CONV_W))
    return out, full[:, -(CONV_W - 1):]


def _lin_combine(e1, e2):
    a1, b1 = e1
    a2, b2 = e2
    return a1 * a2, a2 * b1 + b2


def rg_lru(xc, h0, wa, ba, wx, bx, lam):
    B, T, W = xc.shape
    xb = xc.reshape(B, T, LRU_BLOCKS, LRU_BLOCK)
    r = jax.nn.sigmoid((jnp.einsum('btnd,nde->btne', xb, wa).reshape(B, T, W) + ba).astype(jnp.float32))
    i = jax.nn.sigmoid((jnp.einsum('btnd,nde->btne', xb, wx).reshape(B, T, W) + bx).astype(jnp.float32))
    log_a = -RG_C * r * jax.nn.softplus(-lam.astype(jnp.float32))
    a = jnp.exp(log_a)
    u = jnp.sqrt(-jnp.expm1(2.0 * log_a)) * (i * xc.astype(jnp.float32))
    u = u.at[:, 0].add(a[:, 0] * h0.astype(jnp.float32))
    _, h = lax.associative_scan(_lin_combine, (a, u), axis=1)
    return h.astype(xc.dtype), h[:, -1].astype(xc.dtype)


def rel_bias_lookup(table, tq, tk, offset):
    rel = offset + jnp.arange(tq)[:, None] - jnp.arange(tk)[None, :]
    idx = jnp.clip(rel, -MAX_REL, MAX_REL) + MAX_REL
    return jnp.transpose(table[idx], (2, 0, 1)).astype(jnp.float32)


def band_attention_prompt(q, k, v, table):
    B, S, H, Dh = q.shape
    nc = S // CHUNK
    pad = ((0, 0), (BAND, 0), (0, 0), (0, 0))
    kc = jnp.pad(k, pad).reshape(B, nc + PAST_CHUNKS, CHUNK, H, Dh)
    vc = jnp.pad(v, pad).reshape(B, nc + PAST_CHUNKS, CHUNK, H, Dh)
    kb = jnp.concatenate([kc[:, j:j + nc] for j in range(PAST_CHUNKS + 1)], axis=2)
    vb = jnp.concatenate([vc[:, j:j + nc] for j in range(PAST_CHUNKS + 1)], axis=2)
    qc = q.reshape(B, nc, CHUNK, H, Dh)
    nk = (PAST_CHUNKS + 1) * CHUNK
    s = jnp.einsum('bnqhd,bnkhd->bnhqk', qc, kb).astype(jnp.float32) * (HEAD_DIM ** -0.5)
    s = s + rel_bias_lookup(table, CHUNK, nk, BAND)[None, None]
    key_pos = jnp.arange(nc)[:, None] * CHUNK + jnp.arange(nk)[None, :] - BAND
    s = jnp.where((key_pos >= 0)[None, :, None, None, :], s, NEG)
    pr = jax.nn.softmax(s, axis=-1).astype(v.dtype)
    o = jnp.einsum('bnhqk,bnkhd->bnqhd', pr, vb)
    return o.reshape(B, S, H * Dh)


def band_attention_sample(q, k_new, v_new, k_cache, v_cache, table):
    B, T, H, Dh = q.shape
    L = k_cache.shape[1]
    kk = jnp.concatenate([k_cache.astype(k_new.dtype), k_new], axis=1)
    vv = jnp.concatenate([v_cache.astype(v_new.dtype), v_new], axis=1)
    s = jnp.einsum('bqhd,bkhd->bhqk', q, kk).astype(jnp.float32) * (HEAD_DIM ** -0.5)
    s = s + rel_bias_lookup(table, T, L + T, L)[None]
    pr = jax.nn.softmax(s, axis=-1).astype(vv.dtype)
    o = jnp.einsum('bhqk,bkhd->bqhd', pr, vv)
    return o.reshape(B, T, H * Dh)


def layer_front(x, norm_g, w_in, q_g, k_g):
    B, T, _ = x.shape
    z = rms_norm(x, norm_g) @ w_in
    xl, gl, q, k, v, ga = jnp.split(z, SPLITS, axis=-1)
    q = rms_norm(q.reshape(B, T, N_HEADS, HEAD_DIM), q_g)
    k = rms_norm(k.reshape(B, T, N_HEADS, HEAD_DIM), k_g)
    v = v.reshape(B, T, N_HEADS, HEAD_DIM)
    return xl, gl, q, k, v, ga


def layer_back(x, lru, gl, att, ga, w_out, p, ple_g, w_pg, w_pe):
    mix = jnp.concatenate([lru * jax.nn.silu(gl), att * jax.nn.silu(ga)], axis=-1) @ w_out
    h = x + mix
    gate = jax.nn.sigmoid(rms_norm(h, ple_g) @ w_pg)
    return h + (p @ w_pe) * gate


def setup_inputs(seed: int = 0) -> dict:
    key = jax.random.key(seed)
    ks = jax.random.split(key, 32)
    nrm = lambda k, s, sc: jax.random.normal(k, s, jnp.float32) * sc
    keep_s = min(BAND, PAST_LEN)
    u = jax.random.uniform(ks[20], (DEPTH, W_LRU), jnp.float32, 0.9, 0.999) ** (1.0 / RG_C)
    return {
        'x_prompt': nrm(ks[0], (BATCH, SEQ, D_MODEL), 1.0),
        'x_sample': nrm(ks[1], (DEC_BATCH, DEC_SEQ, D_MODEL), 1.0),
        'p_prompt': nrm(ks[2], (DEPTH, BATCH, SEQ, PLE_DIM), 1.0),
        'p_sample': nrm(ks[3], (DEPTH, DEC_BATCH, DEC_SEQ, PLE_DIM), 1.0),
        'cache_k': nrm(ks[4], (DEPTH, DEC_BATCH, keep_s, N_HEADS, HEAD_DIM), 1.0),
        'cache_v': nrm(ks[5], (DEPTH, DEC_BATCH, keep_s, N_HEADS, HEAD_DIM), 1.0),
        'state_conv': nrm(ks[6], (DEPTH, DEC_BATCH, CONV_W - 1, W_LRU), 1.0),
        'state_lru': nrm(ks[7], (DEPTH, DEC_BATCH, W_LRU), 0.5),
        'norm_g': 1.0 + nrm(ks[8], (DEPTH, D_MODEL), 0.01),
        'w_in': nrm(ks[9], (DEPTH, D_MODEL, IN_WIDTH), D_MODEL ** -0.5),
        'conv_w': nrm(ks[10], (DEPTH, CONV_W, W_LRU), CONV_W ** -0.5),
        'conv_b': nrm(ks[11], (DEPTH, W_LRU), 0.01),
        'gate_a_w': nrm(ks[12], (DEPTH, LRU_BLOCKS, LRU_BLOCK, LRU_BLOCK), LRU_BLOCK ** -0.5),
        'gate_a_b': nrm(ks[13], (DEPTH, W_LRU), 0.01),
        'gate_x_w': nrm(ks[14], (DEPTH, LRU_BLOCKS, LRU_BLOCK, LRU_BLOCK), LRU_BLOCK ** -0.5),
        'gate_x_b': nrm(ks[15], (DEPTH, W_LRU), 0.01),
        'lru_lambda': jnp.log(u / (1.0 - u)),
        'q_norm_g': 1.0 + nrm(ks[16], (DEPTH, HEAD_DIM), 0.01),
        'k_norm_g': 1.0 + nrm(ks[17], (DEPTH, HEAD_DIM), 0.01),
        'rel_bias': nrm(ks[18], (DEPTH, 2 * MAX_REL + 1, N_HEADS), 0.1),
        'w_out': nrm(ks[19], (DEPTH, W_LRU + W_ATT, D_MODEL), (W_LRU + W_ATT) ** -0.5),
        'ple_norm_g': 1.0 + nrm(ks[21], (DEPTH, D_MODEL), 0.01),
        'w_ple_gate': nrm(ks[22], (DEPTH, D_MODEL, D_MODEL), D_MODEL ** -0.5),
        'w_ple_proj': nrm(ks[23], (DEPTH, PLE_DIM, D_MODEL), PLE_DIM ** -0.5),
    }


def reference(x_prompt, x_sample, p_prompt, p_sample, cache_k, cache_v, state_conv, state_lru,
              norm_g, w_in, conv_w, conv_b, gate_a_w, gate_a_b, gate_x_w, gate_x_b, lru_lambda,
              q_norm_g, k_norm_g, rel_bias, w_out, ple_norm_g, w_ple_gate, w_ple_proj):
    yp, ys = x_prompt, x_sample
    B, S, _ = x_prompt.shape
    keep_p = min(BAND, S)
    pk, pv, pc, ph, sk, sv, sc, sh = [], [], [], [], [], [], [], []
    for l in range(DEPTH):
        xl, gl, q, k, v, ga = layer_front(yp, norm_g[l], w_in[l], q_norm_g[l], k_norm_g[l])
        xc, cbuf = causal_conv(xl, jnp.zeros((B, CONV_W - 1, W_LRU), xl.dtype), conv_w[l], conv_b[l])
        lru, h_last = rg_lru(xc, jnp.zeros((B, W_LRU), xl.dtype), gate_a_w[l], gate_a_b[l],
                             gate_x_w[l], gate_x_b[l], lru_lambda[l])
        att = band_attention_prompt(q, k, v, rel_bias[l])
        yp_next = layer_back(yp, lru, gl, att, ga, w_out[l], p_prompt[l], ple_norm_g[l],
                             w_ple_gate[l], w_ple_proj[l])
        pk.append(k[:, S - keep_p:]); pv.append(v[:, S - keep_p:]); pc.append(cbuf); ph.append(h_last)
        xl, gl, q, k, v, ga = layer_front(ys, norm_g[l], w_in[l], q_norm_g[l], k_norm_g[l])
        xc, cbuf = causal_conv(xl, state_conv[l], conv_w[l], conv_b[l])
        lru, h_last = rg_lru(xc, state_lru[l], gate_a_w[l], gate_a_b[l],
                             gate_x_w[l], gate_x_b[l], lru_lambda[l])
        att = band_attention_sample(q, k, v, cache_k[l], cache_v[l], rel_bias[l])
        ys_next = layer_back(ys, lru, gl, att, ga, w_out[l], p_sample[l], ple_norm_g[l],
                             w_ple_gate[l], w_ple_proj[l])
        sk.append(k); sv.append(v); sc.append(cbuf); sh.append(h_last)
        yp, ys = yp_next, ys_next
    return (yp, ys, jnp.stack(pk), jnp.stack(pv), jnp.stack(pc), jnp.stack(ph),
            jnp.stack(sk), jnp.stack(sv), jnp.stack(sc), jnp.stack(sh))
```

```python
import os
import numpy as np
import concourse.bass as bass
import concourse.mybir as mybir
from concourse.bass_utils import run_bass_kernel_spmd
from contextlib import ExitStack

F32 = mybir.dt.float32
BF16 = mybir.dt.bfloat16
AF = mybir.ActivationFunctionType
ALU = mybir.AluOpType

NCORES = 8
D = 1024
S = 2048
NSEQ = 4
TS = 32
EPS = 1e-6
WZ = 768


class Buf:
    __slots__ = ("w", "r", "dsem", "dcnt", "name")

    def __init__(self, name=""):
        self.w = None
        self.r = []
        self.dsem = {}
        self.dcnt = {}
        self.name = name


class Eng:
    def __init__(self, h, sem, kind):
        self.h = h
        self.sem = sem
        self.kind = kind
        self.cnt = 0
        self.waited = {}


class K:
    def __init__(self, nc, es):
        self.nc = nc
        self.es = es
        self.nsem = 0
        mk = lambda h, kind: Eng(h, self.sem(), kind)
        self.pe = mk(nc.tensor, "pe")
        self.act = mk(nc.scalar, "cmp")
        self.dve = mk(nc.vector, "cmp")
        self.pool = mk(nc.gpsimd, "cmp")
        self.sp = Eng(nc.sync, None, "dma")
        self.pq = Eng(nc.gpsimd, None, "dma")
        self.out_toks = []

    def sem(self):
        self.nsem += 1
        return self.es.enter_context(self.nc.semaphore("s%d" % self.nsem))

    def sb(self, name, shape, dt):
        return self.es.enter_context(self.nc.sbuf_tensor(name, shape, dt))

    def ps(self, name, shape, dt):
        return self.es.enter_context(self.nc.psum_tensor(name, shape, dt))

    def _deps(self, eng, reads, writes):
        deps = []
        for b in reads:
            if b.w is not None:
                deps.append(b.w)
        for b in writes:
            if b.w is not None:
                deps.append(b.w)
            deps.extend(b.r)
        for (sem, val) in deps:
            if sem is eng.sem:
                if eng.kind == "pe":
                    continue
            if eng.waited.get(id(sem), 0) >= val:
                continue
            eng.h.wait_ge(sem, val)
            eng.waited[id(sem)] = val

    def _mark(self, tok, reads, writes):
        for b in reads:
            b.r.append(tok)
        for b in writes:
            b.w = tok
            b.r = []

    def op(self, eng, fn, reads=(), writes=()):
        self._deps(eng, reads, writes)
        inst = fn(eng.h)
        eng.cnt += 1
        inst.then_inc(eng.sem, 1)
        tok = (eng.sem, eng.cnt)
        self._mark(tok, reads, writes)
        return tok

    def grp(self, fns, reads=(), writes=()):
        eng = self.pe
        self._deps(eng, reads, writes)
        inst = None
        for fn in fns:
            inst = fn(eng.h)
        eng.cnt += 1
        inst.then_inc(eng.sem, 1)
        tok = (eng.sem, eng.cnt)
        self._mark(tok, reads, writes)
        return tok

    def dma(self, q, out, in_, sbuf, reads=(), writes=(), is_out=False):
        self._deps(q, reads, writes)
        qk = id(q)
        if qk not in sbuf.dsem:
            sbuf.dsem[qk] = self.sem()
            sbuf.dcnt[qk] = 0
        inst = q.h.dma_start(out=out, in_=in_)
        sbuf.dcnt[qk] += 16
        inst.then_inc(sbuf.dsem[qk], 16)
        tok = (sbuf.dsem[qk], sbuf.dcnt[qk])
        self._mark(tok, reads, writes)
        if is_out:
            self.out_toks.append(tok)
        return tok

    def finish(self):
        last = {}
        for (sem, val) in self.out_toks:
            if last.get(id(sem), (None, 0))[1] < val:
                last[id(sem)] = (sem, val)
        for (sem, val) in last.values():
            self.sp.h.wait_ge(sem, val)


class Ring:
    def __init__(self, items):
        self.items = items
        self.i = 0

    def get(self):
        it = self.items[self.i % len(self.items)]
        self.i += 1
        return it


def build_nc(STAGE=99):
    FLG = os.environ.get('MK_FLG', 'kvp')
    SUB = os.environ.get('MK_SUB', 'z')
    nc = bass.Bass("TRN2", target_bir_lowering=False)
    di = lambda n, s: nc.dram_tensor(n, s, F32, kind="ExternalInput").ap()
    do = lambda n, s: nc.dram_tensor(n, s, F32, kind="ExternalOutput").ap()
    xp = di("xp", [S, D]); xs = di("xs", [128, D])
    pp = di("pp", [S, 256]); pss = di("pss", [128, 256])
    ck = di("ck", [NSEQ, 512, 512]); cv = di("cv", [NSEQ, 512, 512])
    sconv = di("sconv", [128, 4, NSEQ, 3]); slru = di("slru", [128, 4, NSEQ])
    w_in = di("w_in", [D, 3072]); w_out = di("w_out", [D, D]); w_pg = di("w_pg", [D, D]); w_pe = di("w_pe", [256, D])
    ng = di("ng", [1, D]); pg = di("pg", [1, D])
    chanv = di("chanv", [128, 8, 4]); qkg = di("qkg", [128, 2])
    wab = di("wab", [128, 4, 128]); wxb = di("wxb", [128, 4, 128])
    relb = di("relb", [8, 513])
    yp = do("yp", [S, D]); ys = do("ys", [128, D])
    pk = do("pk", [512, 512]); pv = do("pv", [512, 512])
    pc_t = do("pc_t", [128, 4, 3]); ph_t = do("ph_t", [128, 4])
    sk = do("sk", [128, 512]); sv = do("sv", [128, 512])
    sc_t = do("sc_t", [128, 4, NSEQ, 3]); sh_t = do("sh_t", [128, 4, NSEQ])
    zd = nc.dram_tensor("zd", [8, 128, WZ], F32, kind="Internal").ap()

    with ExitStack() as es:
        k = K(nc, es)
        pe, act, dve, pool, sp, pq = k.pe, k.act, k.dve, k.pool, k.sp, k.pq

        w_in_sb = k.sb("w_in_sb", [128, 8, 3072], BF16)
        w_out_sb = k.sb("w_out_sb", [128, 8, D], BF16)
        w_pg_sb = k.sb("w_pg_sb", [128, 8, D], BF16)
        w_pe_sb = k.sb("w_pe_sb", [128, 2, D], BF16)
        B_win = [Buf("win%d" % g) for g in range(24)]
        B_wout, B_wpg, B_wpe = Buf("wout"), Buf("wpg"), Buf("wpe")

        identf = k.sb("identf", [128, 128], F32); ident = k.sb("ident", [128, 128], BF16)
        onesb = k.sb("onesb", [128, 128], BF16)
        dconv = k.sb("dconv", [128, 4, 4, 128], BF16)
        wab_sb = k.sb("wab_sb", [128, 4, 128], BF16); wxb_sb = k.sb("wxb_sb", [128, 4, 128], BF16)
        ng_bc = k.sb("ng_bc", [128, D], F32); pg_bc = k.sb("pg_bc", [128, D], F32)
        chv = k.sb("chv", [128, 8, 4], F32)
        qkg_sb = k.sb("qkg_sb", [128, 2], F32)
        nsp = k.sb("nsp", [128, 8], F32)
        epst = k.sb("epst", [128, 1], F32)
        EB = k.sb("EB", [128, 8, 640], BF16)
        EBn = k.sb("EBn", [128, NSEQ, 8, TS], BF16)
        B_const = Buf("const")
        B_wg = Buf("wg")
        B_gbc = Buf("gbc")
        B_EB = Buf("EB")

        ssq = k.sb("ssq", [128, 17], F32); rstd = k.sb("rstd", [128, 17], F32)
        pssq = k.sb("pssq", [128, 17], F32); prstd = k.sb("prstd", [128, 17], F32)
        B_ssq = [Buf() for _ in range(17)]; B_pssq = [Buf() for _ in range(17)]

        NX = 3
        xpool = [(k.sb("xt%d" % i, [128, D], F32), Buf("xt%d" % i)) for i in range(NX)]
        xring = Ring(xpool)
        xnr = Ring([(k.sb("xn%d" % i, [128, D], BF16), Buf()) for i in range(2)])
        xl_ext = k.sb("xl_ext", [128, 4, 516], BF16); B_xl = [Buf() for _ in range(4)]
        xl_exs = k.sb("xl_exs", [128, 4, NSEQ, 36], BF16)
        gls = k.sb("gls", [128, 4, 512], BF16); B_gls = [Buf() for _ in range(4)]
        xc32r = Ring([(k.sb("xc32_%d" % i, [128, 512], F32), Buf()) for i in range(2)])
        xcbr = Ring([(k.sb("xcb%d" % i, [128, 512], BF16), Buf()) for i in range(2)])
        T1r = Ring([(k.sb("T1_%d" % i, [128, 512], F32), Buf()) for i in range(1)])
        T2r = Ring([(k.sb("T2_%d" % i, [128, 512], F32), Buf()) for i in range(2)])
        T3r = Ring([(k.sb("T3_%d" % i, [128, 512], F32), Buf()) for i in range(2)])
        T4r = Ring([(k.sb("T4_%d" % i, [128, 512], F32), Buf()) for i in range(2)])
        hsr = Ring([(k.sb("hs%d" % i, [128, 512], F32), Buf()) for i in range(1)])
        carry = k.sb("carry", [128, 4], F32); B_carry = [Buf() for _ in range(4)]
        qrawr = Ring([xc32r.items[0], xc32r.items[1]]); sqr = xcbr; rsr = T4r
        qnT = k.sb("qnT", [128, 4, 512], BF16); B_qnT = [Buf() for _ in range(4)]
        kT = k.sb("kT", [128, 4, 1024], BF16); B_kT = [[Buf() for _ in range(4)] for _ in range(2)]
        kn32l = [T2r.items[0], T2r.items[1], T3r.items[0], T3r.items[1]]
        B_kn32 = [b for (_, b) in kn32l]
        vaug = k.sb("vaug", [128, 8, 8, 65], BF16); B_v = [Buf() for _ in range(8)]
        gas = k.sb("gas", [128, 4, 512], BF16); B_gas = [Buf() for _ in range(4)]
        gass = gas[0:32, :, :]; B_gass = B_gas
        PTr = Ring([(k.sb("PT%d" % i, [128, 512], BF16), Buf()) for i in range(4)])
        PTnr = Ring([(k.sb("PTn%d" % i, [128, 128], BF16), Buf()) for i in range(2)])
        rden = k.sb("rden", [128, 8], F32); B_rden = Buf()
        omr = Ring([(k.sb("om%d" % i, [128, 512], BF16), Buf()) for i in range(2)])
        mixT = k.sb("mixT", [128, 8, 512], BF16); B_mix = [Buf() for _ in range(4)]
        xnT = mixT
        ckb = kT[:, :, 512:1024]; L_ckb = B_kT[1]
        ckT = gls; L_ckT = B_gls
        cvaug = vaug[:, 4:8, :, :]; L_cva = B_v[4:8]
        hnr = xnr
        hnTr = Ring([(k.sb("hnT%d" % i, [128, 8, 128], BF16), Buf()) for i in range(2)])
        gater = Ring([(k.sb("gate%d" % i, [128, 512], F32), Buf()) for i in range(2)])
        voutr = gater; koutr = gater
        o32, B_o32 = gater.items[1]
        pbr = Ring([(k.sb("pb%d" % i, [128, 256], BF16), Buf()) for i in range(2)])
        pTr = Ring([(k.sb("pT%d" % i, [128, 2, 128], BF16), Buf()) for i in range(1)])
        pc32 = k.sb("pc32", [128, 4, 3], F32); ph32 = k.sb("ph32", [128, 4], F32)
        sc32 = k.sb("sc32", [128, 4, NSEQ, 3], F32); sh32 = k.sb("sh32", [128, 4, NSEQ], F32)
        sconv_sb = k.sb("sconv_sb", [128, 4, NSEQ, 3], F32); slru_sb = k.sb("slru_sb", [128, 4, NSEQ], F32)
        B_pc, B_ph, B_sc, B_sh, B_sst = Buf(), Buf(), Buf(), Buf(), Buf()
        FsbA = gater.items[0][0][0:8, 0:384]; B_FA = gater.items[0][1]
        FsbB = gater.items[1][0][0:8, 0:384]; B_FB = gater.items[1][1]
        B_zd = Buf("zd")

        banks = [(k.ps("bank%d" % i, [128, 512], F32), Buf("bank%d" % i)) for i in range(7)]
        trb = (k.ps("ptr", [128, 1024], BF16), Buf("banktr"))
        mmr = Ring([banks[0], banks[1], banks[4], banks[5], banks[6], banks[2], banks[3]]); gabr = mmr; pvb = [banks[2], banks[3]]; scpr = Ring([(banks[4], banks[5]), (banks[6], banks[1])]); mm2 = Ring(banks[0:7])

        def x_src(tile):
            return xp[tile * 128:(tile + 1) * 128, :] if tile < 16 else xs

        def front_load(tile):
            xt, Bx = xring.get()
            k.dma(sp, xt[:], x_src(tile), Bx, writes=[Bx])
            return (xt, Bx)

        B_c2 = Buf("c2")
        k.op(pool, lambda e: e.memset(identf[:], 0.0), writes=[B_c2])
        k.op(pool, lambda e: e.affine_select(out=identf[:], in_=identf[:], pattern=[[-1, 128]], compare_op=ALU.not_equal,
                                              fill=1.0, base=0, channel_multiplier=1), writes=[B_c2])
        k.op(pool, lambda e: e.memset(onesb[:], 0.0), writes=[B_c2])
        k.op(pool, lambda e: e.memset(onesb[0:64, 0:64], 1.0 / 64), writes=[B_c2])
        k.op(pool, lambda e: e.memset(onesb[64:128, 64:128], 1.0 / 64), writes=[B_c2])
        k.op(pool, lambda e: e.memset(epst[:], EPS), writes=[B_c2])
        k.op(pool, lambda e: e.memset(ssq[:], 0.0), writes=[B_c2])
        k.op(pool, lambda e: e.memset(pssq[:], 0.0), writes=[B_c2])
        k.op(pool, lambda e: e.memset(carry[:], 0.0), writes=[B_c2] + B_carry)
        k.op(pool, lambda e: e.memset(xl_ext[:, :, 0:4], 0.0), writes=B_xl)
        k.op(pool, lambda e: e.memset(vaug[:, :, :, 64:65], 1.0), writes=B_v)
        w_in_v = w_in.rearrange("(k p) n -> p k n", p=128)
        for g in list(range(8, 24)) + list(range(0, 8)):
            k.dma(pq, w_in_sb[:, :, g * 128:(g + 1) * 128], w_in_v[:, :, g * 128:(g + 1) * 128], B_win[g], writes=[B_win[g]])
        B_c_list = [Buf() for _ in range(4)]
        k.dma(sp, chv[:], chanv, B_c_list[0], writes=[B_c_list[0]])
        k.dma(sp, qkg_sb[:], qkg, B_c_list[1], writes=[B_c_list[1]])
        k.dma(sp, sconv_sb[:], sconv, B_c_list[2], writes=[B_c_list[2]])
        k.dma(sp, slru_sb[:], slru, B_c_list[3], writes=[B_c_list[3]])
        B_g1, B_g2 = Buf(), Buf()
        k.dma(sp, ng_bc[:], ng.to_broadcast([128, D]), B_g1, writes=[B_g1])
        k.dma(sp, pg_bc[:], pg.to_broadcast([128, D]), B_g2, writes=[B_g2])
        preload = {t: front_load(t) for t in range(3)}
        k.dma(sp, FsbA, relb[:, 129:513], B_FA, writes=[B_FA])
        k.op(dve, lambda e: e.tensor_copy(out=FsbB, in_=FsbA[:, 383:384].to_broadcast([8, 384])), reads=[B_FA], writes=[B_FB])
        k.dma(sp, zd[:, 0:8, 0:384], FsbA.unsqueeze(1).to_broadcast([8, 8, 384]), B_zd, reads=[B_FA], writes=[B_zd])
        k.dma(sp, zd[:, 0:8, 384:WZ], FsbB.unsqueeze(1).to_broadcast([8, 8, 384]), B_zd, reads=[B_FB], writes=[B_zd])
        nrow = 8
        while nrow < 128:
            k.dma(sp, zd[:, nrow:2 * nrow, :], zd[:, 0:nrow, :], B_zd, reads=[B_zd], writes=[B_zd])
            nrow *= 2
        k.op(dve, lambda e: e.memset(rden[:, 0:1], 0.0), reads=B_c_list, writes=[B_const])
        B_wg2 = Buf()
        k.dma(pq, wab_sb[:], wab, B_wg, writes=[B_wg])
        k.dma(pq, wxb_sb[:], wxb, B_wg2, writes=[B_wg2])
        k.dma(pq, w_out_sb[:], w_out.rearrange("(k p) n -> p k n", p=128), B_wout, writes=[B_wout])
        k.dma(pq, w_pg_sb[:], w_pg.rearrange("(k p) n -> p k n", p=128), B_wpg, writes=[B_wpg])
        k.dma(pq, w_pe_sb[:], w_pe.rearrange("(k p) n -> p k n", p=128), B_wpe, writes=[B_wpe])

        k.op(dve, lambda e: e.tensor_copy(out=ident[:], in_=identf[:]), reads=[B_c2], writes=[B_c2])
        for t in range(4):
            for c in range(4):
                k.op(dve, lambda e, t=t, c=c: e.tensor_scalar(out=dconv[:, t, c, :], in0=identf[:], scalar1=chv[:, t, c:c + 1],
                                                               scalar2=None, op0=ALU.mult), reads=[B_const, B_c2], writes=[B_c2])
        k.op(dve, lambda e: e.tensor_copy(out=xl_exs[:, :, :, 0:3], in_=sconv_sb[:]), reads=[B_const], writes=B_xl)
        k.op(act, lambda e: e.activation(out=nsp[:, 0:4], in_=chv[:, 7, :], func=AF.Exp, scale=-1.0), reads=[B_const], writes=[B_c2])
        k.op(act, lambda e: e.activation(out=nsp[:, 0:4], in_=nsp[:, 0:4], func=AF.Ln, bias=1.0), reads=[B_c2], writes=[B_c2])
        k.op(dve, lambda e: e.tensor_scalar(out=nsp[:, 4:8], in0=nsp[:, 0:4], scalar1=-16.0, scalar2=None, op0=ALU.mult), reads=[B_c2], writes=[B_c2])
        k.op(dve, lambda e: e.tensor_scalar(out=nsp[:, 0:4], in0=nsp[:, 0:4], scalar1=-8.0, scalar2=None, op0=ALU.mult), reads=[B_c2], writes=[B_c2])
        def build_EB():
            stg = [gater.items[0], gater.items[1], T2r.items[0], T2r.items[1]]
            for h in range(8):
                for half in range(2):
                    t, B = stg[(2 * h + half) % 4]
                    src = bass.AP(zd.tensor, h * 128 * WZ + 127 + half * 320, [[WZ - 1, 128], [1, 320]])
                    k.dma(sp, t[:, 0:320], src, B, reads=[B_zd], writes=[B])
                    k.op(act, lambda e, h=h, half=half, t=t: e.activation(out=EB[:, h, half * 320:(half + 1) * 320], in_=t[:, 0:320], func=AF.Exp),
                         reads=[B], writes=[B_EB])
            for g in range(2):
                t, B = T3r.items[g]
                tv = t[:].rearrange("p (s h q) -> p s h q", s=2, h=8)
                k.op(dve, lambda e, t=t: e.memset(t[:], -30000.0), writes=[B])
                for s2 in range(2):
                    s_ = 2 * g + s2
                    src = bass.AP(zd.tensor, 127, [[WZ - 1, TS], [128 * WZ, 8], [1, TS]])
                    k.dma(sp, tv[s_ * TS:(s_ + 1) * TS, s2, :, :], src, B, reads=[B_zd], writes=[B])
                k.op(act, lambda e, g=g, tv=tv: e.activation(out=EBn[:, 2 * g:2 * g + 2, :, :], in_=tv, func=AF.Exp), reads=[B], writes=[B_EB])
            k.op(pool, lambda e: e.memset(EB[0:64, :, 512 + 64:640], 0.0), reads=[], writes=[B_EB])
            k.op(pool, lambda e: e.memset(EB[64:128, :, 0:64], 0.0), reads=[], writes=[B_EB])

        fr_state = {}

        def front_a(tile, jl):
            xt, Bx = preload.pop(tile) if tile in preload else front_load(tile)
            xn, Bxn = xnr.get()
            fr_state[tile] = (xn, Bxn)
            k.op(act, lambda e: e.activation(out=xn[:], in_=xt[:], func=AF.Square, accum_out=ssq[:, tile:tile + 1]),
                 reads=[Bx, B_c2], writes=[Bxn, B_ssq[tile]])
            k.op(act, lambda e: e.activation(out=rstd[:, tile:tile + 1], in_=ssq[:, tile:tile + 1], func=AF.Sqrt, scale=1.0 / D, bias=epst[:]),
                 reads=[B_ssq[tile]], writes=[B_ssq[tile]])
            k.op(dve, lambda e: e.reciprocal(out=rstd[:, tile:tile + 1], in_=rstd[:, tile:tile + 1]), reads=[B_ssq[tile]], writes=[B_ssq[tile]])
            k.op(dve, lambda e: e.scalar_tensor_tensor(out=xn[:], in0=xt[:], scalar=rstd[:, tile:tile + 1], in1=ng_bc[:], op0=ALU.mult, op1=ALU.mult),
                 reads=[Bx, B_ssq[tile], B_g1], writes=[Bxn])

        def front_b(tile, jl):
            xn, Bxn = fr_state.pop(tile)
            tp, Bt = trb
            k.grp([lambda e, c=c: e.transpose(out=tp[:, c * 128:(c + 1) * 128], in_=xn[:, c * 128:(c + 1) * 128], identity=ident[:]) for c in range(8)],
                  reads=[Bxn, B_c2], writes=[Bt])
            k.op(act, lambda e: e.activation(out=xnT[:, :, jl * 128:(jl + 1) * 128], in_=tp[:].rearrange("p (c t) -> p c t", c=8), func=AF.Copy),
                 reads=[Bt], writes=[B_mix[jl]])

        def mm_fm(f, N):
            pt, Bp = mmr.get()
            k.grp([lambda e, kc=kc: e.matmul(pt[:, 0:N], lhsT=w_in_sb[:, kc, f * 128:(f + 1) * 128], rhs=xnT[:, kc, 0:N],
                                             start=(kc == 0), stop=(kc == 7)) for kc in range(8)],
                  reads=B_mix + [B_win[f]], writes=[Bp])
            return pt, Bp

        def qk1(f, N):
            pt, Bp = mm_fm(f, N)
            sq, Bs = sqr.get()
            k.op(act, lambda e: e.activation(out=sq[:, 0:N], in_=pt[:, 0:N], func=AF.Square), reads=[Bp], writes=[Bs])
            return (pt, Bp, sq, Bs)

        def qk2(f, N, blk, is_k, st):
            c = f % 4
            qr, Bq, sq, Bs = st
            rs, Br = rsr.get()
            p2, Bp2 = mmr.get()
            k.grp([lambda e: e.matmul(p2[:, 0:N], lhsT=onesb[:], rhs=sq[:, 0:N], start=True, stop=True)], reads=[Bs, B_c2], writes=[Bp2])
            k.op(act, lambda e: e.activation(out=rs[:, 0:N], in_=p2[:, 0:N], func=AF.Ln, bias=epst[:]), reads=[Bp2, B_c2], writes=[Br])
            k.op(act, lambda e: e.activation(out=rs[:, 0:N], in_=rs[:, 0:N], func=AF.Exp, scale=-0.5), reads=[Br], writes=[Br])
            gcol = qkg_sb[:, 1:2] if is_k else qkg_sb[:, 0:1]
            if not is_k:
                k.op(dve, lambda e: e.scalar_tensor_tensor(out=qnT[:, c, 0:N], in0=qr[:, 0:N], scalar=gcol, in1=rs[:, 0:N], op0=ALU.mult, op1=ALU.mult),
                     reads=[Bq, Br, B_const], writes=[B_qnT[c]])
            else:
                slot = blk % 2
                if blk >= 3 and 'k' in FLG:
                    kn = kn32l[c][0]
                    k.op(dve, lambda e: e.scalar_tensor_tensor(out=kn[:, 0:N], in0=qr[:, 0:N], scalar=gcol, in1=rs[:, 0:N], op0=ALU.mult, op1=ALU.mult),
                         reads=[Bq, Br, B_const], writes=[B_kn32[c]])
                    k.op(pool, lambda e: e.tensor_copy(out=kT[:, c, slot * 512:slot * 512 + N], in_=kn[:, 0:N]), reads=[B_kn32[c]], writes=[B_kT[slot][c]])
                else:
                    k.op(dve, lambda e: e.scalar_tensor_tensor(out=kT[:, c, slot * 512:slot * 512 + N], in0=qr[:, 0:N], scalar=gcol, in1=rs[:, 0:N],
                                                               op0=ALU.mult, op1=ALU.mult), reads=[Bq, Br, B_const], writes=[B_kT[slot][c]])

        def mm_tm(jl, col0, M0, M):
            pt, Bp = mmr.get()
            g = col0 // 128
            k.grp([lambda e, kc=kc: e.matmul(pt[0:M, :], lhsT=xnT[:, kc, M0:M0 + M], rhs=w_in_sb[:, kc, col0:col0 + 512],
                                             start=(kc == 0), stop=(kc == 7)) for kc in range(8)],
                  reads=B_mix + B_win[g:g + 4], writes=[Bp])
            return pt, Bp

        ga_state = {}

        def group_a(c, N, blk):
            ga1(c, N, blk)
            ga2(c, N, blk)

        def ga1(c, N, blk):
            sample = blk == 4
            pt, Bp = mmr.get()
            if not sample:
                k.grp([lambda e, t=t: e.matmul(pt[:, 0:N], lhsT=dconv[:, t, c, :], rhs=xl_ext[:, c, t:t + N], start=(t == 0), stop=(t == 3)) for t in range(4)],
                      reads=[B_xl[c], B_c2], writes=[Bp])
            else:
                fns = []
                for s in range(NSEQ):
                    for t in range(4):
                        fns.append(lambda e, t=t, s=s: e.matmul(pt[:, s * TS:(s + 1) * TS], lhsT=dconv[:, t, c, :], rhs=xl_exs[:, c, s, t:t + TS],
                                                                 start=(t == 0), stop=(t == 3)))
                k.grp(fns, reads=[B_xl[c], B_c2], writes=[Bp])
            xc, Bxc = xc32r.get(); xb, Bxb = xcbr.get()
            k.op(act, lambda e: e.activation(out=xc[:, 0:N], in_=pt[:, 0:N], func=AF.Identity, bias=chv[:, 4, c:c + 1]), reads=[Bp, B_const], writes=[Bxc])
            k.op(pool, lambda e: e.tensor_copy(out=xb[:, 0:N], in_=xc[:, 0:N]), reads=[Bxc], writes=[Bxb])
            ga_state[c] = (xc, Bxc, xb, Bxb)

        def ga2(c, N, blk):
            sample = blk == 4
            xc, Bxc, xb, Bxb = ga_state.pop(c)
            pr, Bpr = mmr.get()
            k.grp([lambda e: e.matmul(pr[:, 0:N], lhsT=wab_sb[:, c, :], rhs=xb[:, 0:N], start=True, stop=True)], reads=[Bxb, B_wg], writes=[Bpr])
            pi, Bpi = mmr.get()
            k.grp([lambda e: e.matmul(pi[:, 0:N], lhsT=wxb_sb[:, c, :], rhs=xb[:, 0:N], start=True, stop=True)], reads=[Bxb, B_wg2], writes=[Bpi])
            t1, B1 = T1r.get(); t2, B2 = T2r.get(); t3, B3 = T3r.get(); t4, B4 = T4r.get()
            k.op(act, lambda e: e.activation(out=t1[:, 0:N], in_=pr[:, 0:N], func=AF.Sigmoid, bias=chv[:, 5, c:c + 1]), reads=[Bpr, B_const], writes=[B1])
            k.op(act, lambda e: e.activation(out=t2[:, 0:N], in_=pi[:, 0:N], func=AF.Sigmoid, bias=chv[:, 6, c:c + 1]), reads=[Bpi, B_const], writes=[B2])
            k.op(act, lambda e: e.activation(out=t3[:, 0:N], in_=t1[:, 0:N], func=AF.Exp, scale=nsp[:, c:c + 1]), reads=[B1, B_c2], writes=[B3])
            k.op(act, lambda e: e.activation(out=t4[:, 0:N], in_=t1[:, 0:N], func=AF.Exp, scale=nsp[:, 4 + c:5 + c]), reads=[B1, B_c2], writes=[B4])
            k.op(act, lambda e: e.activation(out=t4[:, 0:N], in_=t4[:, 0:N], func=AF.Sqrt, scale=-1.0, bias=1.0), reads=[B4], writes=[B4])
            k.op(dve, lambda e: e.tensor_tensor(out=t2[:, 0:N], in0=t2[:, 0:N], in1=xc[:, 0:N], op=ALU.mult), reads=[B2, Bxc], writes=[B2])
            k.op(dve, lambda e: e.tensor_tensor(out=t2[:, 0:N], in0=t2[:, 0:N], in1=t4[:, 0:N], op=ALU.mult), reads=[B2, B4], writes=[B2])
            hs, Bh = hsr.get()
            if not sample:
                k.op(dve, lambda e: e.tensor_tensor_scan(out=hs[:, 0:N], data0=t3[:, 0:N], data1=t2[:, 0:N], initial=carry[:, c:c + 1], op0=ALU.mult, op1=ALU.add),
                     reads=[B3, B2, B_carry[c]], writes=[Bh])
                k.op(dve, lambda e: e.tensor_copy(out=carry[:, c:c + 1], in_=hs[:, N - 1:N]), reads=[Bh], writes=[B_carry[c]])
                if blk == 3 and 'p' in FLG:
                    k.op(dve, lambda e: e.tensor_copy(out=ph32[:, c:c + 1], in_=hs[:, N - 1:N]), reads=[Bh], writes=[B_ph])
            else:
                for s in range(NSEQ):
                    k.op(dve, lambda e, s=s: e.tensor_tensor_scan(out=hs[:, s * TS:(s + 1) * TS], data0=t3[:, s * TS:(s + 1) * TS], data1=t2[:, s * TS:(s + 1) * TS],
                                                                  initial=slru_sb[:, c, s:s + 1], op0=ALU.mult, op1=ALU.add),
                         reads=[B3, B2, B_const], writes=[Bh])
                k.op(dve, lambda e: e.tensor_copy(out=sh32[:, c, :], in_=hs[:, 0:N].rearrange("p (s t) -> p s t", s=NSEQ)[:, :, TS - 1]), reads=[Bh], writes=[B_sh])
            ntile = N // 128
            k.op(dve, lambda e: e.tensor_tensor(out=mixT[:, c, 0:N], in0=hs[:, 0:N], in1=gls[:, c, 0:N], op=ALU.mult),
                 reads=[Bh, B_gls[c]], writes=B_mix[0:ntile])

        def ga_pair(cs, N, blk, defer=False, to_gls=False):
            sample = blk == 4
            tail = []

            def T(eng, fn, reads=(), writes=()):
                if defer:
                    tail.append(lambda: k.op(eng, fn, reads=reads, writes=writes))
                else:
                    k.op(eng, fn, reads=reads, writes=writes)
            st = {}
            for c in cs:
                pt, Bp = gabr.get()
                if not sample:
                    k.grp([lambda e, t=t: e.matmul(pt[:, 0:N], lhsT=dconv[:, t, c, :], rhs=xl_ext[:, c, t:t + N], start=(t == 0), stop=(t == 3)) for t in range(4)],
                          reads=[B_xl[c], B_c2], writes=[Bp])
                else:
                    fns = []
                    for s_ in range(NSEQ):
                        for t in range(4):
                            fns.append(lambda e, t=t, s_=s_: e.matmul(pt[:, s_ * TS:(s_ + 1) * TS], lhsT=dconv[:, t, c, :], rhs=xl_exs[:, c, s_, t:t + TS],
                                                                       start=(t == 0), stop=(t == 3)))
                    k.grp(fns, reads=[B_xl[c], B_c2], writes=[Bp])
                st[c] = [pt, Bp]
            for c in cs:
                pt, Bp = st[c]
                xc, Bxc = xc32r.get(); xb, Bxb = xcbr.get()
                k.op(act, lambda e: e.activation(out=xc[:, 0:N], in_=pt[:, 0:N], func=AF.Identity, bias=chv[:, 4, c:c + 1]), reads=[Bp, B_const], writes=[Bxc])
                k.op(dve, lambda e: e.tensor_copy(out=xb[:, 0:N], in_=xc[:, 0:N]), reads=[Bxc], writes=[Bxb])
                st[c] = [xc, Bxc, xb, Bxb]
            t1s = [T1r.items[0], gater.items[0]]
            for i, c in enumerate(cs):
                xc, Bxc, xb, Bxb = st[c]
                pr, Bpr = gabr.get()
                k.grp([lambda e: e.matmul(pr[:, 0:N], lhsT=wab_sb[:, c, :], rhs=xb[:, 0:N], start=True, stop=True)], reads=[Bxb, B_wg], writes=[Bpr])
                pi, Bpi = gabr.get()
                k.grp([lambda e: e.matmul(pi[:, 0:N], lhsT=wxb_sb[:, c, :], rhs=xb[:, 0:N], start=True, stop=True)], reads=[Bxb, B_wg2], writes=[Bpi])
                st[c] += [pr, Bpr, pi, Bpi, t1s[i], T2r.get(), T3r.get(), T4r.get()]
            for c in cs:
                xc, Bxc, xb, Bxb, pr, Bpr, pi, Bpi, (t1, B1), (t2, B2), (t3, B3), (t4, B4) = st[c]
                k.op(act, lambda e: e.activation(out=t1[:, 0:N], in_=pr[:, 0:N], func=AF.Sigmoid, bias=chv[:, 5, c:c + 1]), reads=[Bpr, B_const], writes=[B1])
                k.op(act, lambda e: e.activation(out=t2[:, 0:N], in_=pi[:, 0:N], func=AF.Sigmoid, bias=chv[:, 6, c:c + 1]), reads=[Bpi, B_const], writes=[B2])
            for c in cs:
                xc, Bxc, xb, Bxb, pr, Bpr, pi, Bpi, (t1, B1), (t2, B2), (t3, B3), (t4, B4) = st[c]
                T(act, lambda e, t1=t1, t3=t3, c=c: e.activation(out=t3[:, 0:N], in_=t1[:, 0:N], func=AF.Exp, scale=nsp[:, c:c + 1]), reads=[B1, B_c2], writes=[B3])
                T(act, lambda e, t1=t1, t4=t4, c=c: e.activation(out=t4[:, 0:N], in_=t1[:, 0:N], func=AF.Exp, scale=nsp[:, 4 + c:5 + c]), reads=[B1, B_c2], writes=[B4])
            for c in cs:
                xc, Bxc, xb, Bxb, pr, Bpr, pi, Bpi, (t1, B1), (t2, B2), (t3, B3), (t4, B4) = st[c]
                T(act, lambda e, t4=t4: e.activation(out=t4[:, 0:N], in_=t4[:, 0:N], func=AF.Sqrt, scale=-1.0, bias=1.0), reads=[B4], writes=[B4])
            hs, Bh = hsr.get()
            for c in cs:
                xc, Bxc, xb, Bxb, pr, Bpr, pi, Bpi, (t1, B1), (t2, B2), (t3, B3), (t4, B4) = st[c]
                T(dve, lambda e, t2=t2, xc=xc: e.tensor_tensor(out=t2[:, 0:N], in0=t2[:, 0:N], in1=xc[:, 0:N], op=ALU.mult), reads=[B2, Bxc], writes=[B2])
                T(dve, lambda e, t2=t2, t4=t4: e.tensor_tensor(out=t2[:, 0:N], in0=t2[:, 0:N], in1=t4[:, 0:N], op=ALU.mult), reads=[B2, B4], writes=[B2])
                if not sample:
                    T(dve, lambda e, t2=t2, t3=t3, c=c: e.tensor_tensor_scan(out=hs[:, 0:N], data0=t3[:, 0:N], data1=t2[:, 0:N], initial=carry[:, c:c + 1],
                                                                             op0=ALU.mult, op1=ALU.add), reads=[B3, B2, B_carry[c]], writes=[Bh])
                    T(dve, lambda e, c=c: e.tensor_copy(out=carry[:, c:c + 1], in_=hs[:, N - 1:N]), reads=[Bh], writes=[B_carry[c]])
                    if blk == 3 and 'p' in FLG:
                        T(dve, lambda e, c=c: e.tensor_copy(out=ph32[:, c:c + 1], in_=hs[:, N - 1:N]), reads=[Bh], writes=[B_ph])
                else:
                    for s_ in range(NSEQ):
                        T(dve, lambda e, s_=s_, t2=t2, t3=t3, c=c: e.tensor_tensor_scan(out=hs[:, s_ * TS:(s_ + 1) * TS], data0=t3[:, s_ * TS:(s_ + 1) * TS],
                                                                                       data1=t2[:, s_ * TS:(s_ + 1) * TS], initial=slru_sb[:, c, s_:s_ + 1],
                                                                                       op0=ALU.mult, op1=ALU.add), reads=[B3, B2, B_const], writes=[Bh])
                    T(dve, lambda e, c=c: e.tensor_copy(out=sh32[:, c, :], in_=hs[:, 0:N].rearrange("p (s t) -> p s t", s=NSEQ)[:, :, TS - 1]), reads=[Bh], writes=[B_sh])
                if to_gls:
                    T(dve, lambda e, c=c: e.tensor_tensor(out=gls[:, c, 0:N], in0=hs[:, 0:N], in1=gls[:, c, 0:N], op=ALU.mult),
                      reads=[Bh, B_gls[c]], writes=[B_gls[c]])
                else:
                    T(dve, lambda e, c=c: e.tensor_tensor(out=mixT[:, c, 0:N], in0=hs[:, 0:N], in1=gls[:, c, 0:N], op=ALU.mult),
                      reads=[Bh, B_gls[c]], writes=B_mix[0:N // 128])
            return tail

        def attn_epilogue(ppv, Bpv, par, rows):
            pvv = ppv[0:rows, 0:260].rearrange("p (h d) -> p h d", h=4)
            o4 = o32[0:rows, :].rearrange("p (a b d) -> p a b d", a=4, b=2)
            k.op(dve, lambda e: e.reciprocal(out=rden[0:rows, par * 4:par * 4 + 4], in_=pvv[:, :, 64]), reads=[Bpv], writes=[B_rden])
            k.op(dve, lambda e: e.tensor_tensor(out=o4[:, :, par, :], in0=pvv[:, :, 0:64],
                                                in1=rden[0:rows, par * 4:par * 4 + 4].unsqueeze(2).to_broadcast([rows, 4, 64]), op=ALU.mult),
                 reads=[Bpv, B_rden], writes=[B_o32])

        def attn_block(blk, tail):
            items = []
            for p in range(4):
                P = 4 * blk + p
                kts = [kt for kt in range(5) if P - 4 + kt >= 0]
                for kt in kts:
                    items.append((p, kt, kt == kts[0], kt == kts[-1]))
            st = {}
            oms = {}

            def QK(i):
                p, kt, first, last = items[i]
                P = 4 * blk + p
                g0 = 128 * (P - 4 + kt)
                kb = g0 // 512; slot = kb % 2; col = slot * 512 + g0 % 512
                vt = (g0 // 128) % 8
                scp = scpr.get()
                pts = [PTr.get(), PTr.get()]
                st[i] = (scp, pts, vt)
                for par in range(2):
                    psc, Bsc = scp[par]
                    hb = par * 64
                    k.grp([lambda e, a=a: e.matmul(psc[:, a * 128:(a + 1) * 128], lhsT=kT[hb:hb + 64, a, col:col + 128],
                                                   rhs=qnT[hb:hb + 64, a, p * 128:(p + 1) * 128], start=True, stop=True) for a in range(4)],
                          reads=B_kT[slot] + B_qnT, writes=[Bsc])

            def EXP(i):
                p, kt, first, last = items[i]
                scp, pts, vt = st[i]
                for par in range(2):
                    psc, Bsc = scp[par]
                    pt, Bpt = pts[par]
                    k.op(act, lambda e: e.activation(out=pt[:], in_=psc[:], func=AF.Exp, scale=0.125), reads=[Bsc], writes=[Bpt])
                    ebv = EB[:].rearrange("p (a b) m -> p a b m", b=2)[:, :, par, 512 - 128 * kt:640 - 128 * kt]
                    k.op(dve, lambda e: e.tensor_tensor(out=pt[:].rearrange("p (h q) -> p h q", h=4), in0=pt[:].rearrange("p (h q) -> p h q", h=4),
                                                        in1=ebv, op=ALU.mult), reads=[Bpt, B_EB], writes=[Bpt])

            def PV(i):
                p, kt, first, last = items[i]
                scp, pts, vt = st.pop(i)
                for par in range(2):
                    pt, Bpt = pts[par]
                    ppv, Bpv = pvb[par]
                    k.grp([lambda e, a=a: e.matmul(ppv[:, a * 65:(a + 1) * 65], lhsT=pt[:, a * 128:(a + 1) * 128], rhs=vaug[:, vt, 2 * a + par, :],
                                                   start=(first and a == 0), stop=last, skip_group_check=True) for a in range(4)],
                          reads=[Bpt, B_v[vt]], writes=[Bpv])

            def EPI(p):
                om, Bom = omr.get()
                oms[p] = (om, Bom)
                for par in range(2):
                    attn_epilogue(pvb[par][0], pvb[par][1], par, 128)
                k.op(dve, lambda e: e.tensor_tensor(out=om[:], in0=o32[:], in1=gas[:, p, :], op=ALU.mult), reads=[B_o32, B_gas[p]], writes=[Bom])

            def TR(p):
                om, Bom = oms.pop(p)
                tp, Bt = trb
                k.grp([lambda e, c=c: e.transpose(out=tp[:, c * 128:(c + 1) * 128], in_=om[:, c * 128:(c + 1) * 128], identity=ident[:]) for c in range(4)],
                      reads=[Bom, B_c2], writes=[Bt])
                k.op(act, lambda e: e.activation(out=mixT[:, 4:8, p * 128:(p + 1) * 128], in_=tp[:, 0:512].rearrange("p (c t) -> p c t", c=4), func=AF.Copy),
                     reads=[Bt], writes=[B_mix[p]])

            n = len(items)
            QK(0)
            pending = None
            for i in range(n):
                if i + 1 < n:
                    QK(i + 1)
                EXP(i)
                PV(i)
                if pending is not None:
                    TR(pending)
                    pending = None
                if items[i][3]:
                    EPI(items[i][0])
                    pending = items[i][0]
                for _ in range(2):
                    if tail:
                        tail.pop(0)()
            if pending is not None:
                TR(pending)
            while tail:
                tail.pop(0)()

        samp_state = {}

        def samp_load_k(s):
            if s % 2 == 0:
                ckb_s, L = kT[:, :, 512:1024], B_kT[1]
            else:
                ckb_s, L = xl_ext[:, :, 0:512], B_xl
            k.dma(pq, ckb_s, ck[s].rearrange("(t p) f -> p t f", p=128), L[0], writes=L)
            samp_state[("k", s)] = (ckb_s, L)

        def samp_load_v(s):
            vts = []
            for kt in range(4):
                ti = 1 + (4 * s + kt) % 7
                k.dma(pq, vaug[:, ti, :, 0:64], cv[s][kt * 128:(kt + 1) * 128, :].rearrange("p (h d) -> p h d", h=8), B_v[ti], writes=[B_v[ti]])
                vts.append(ti)
            samp_state[("v", s)] = vts

        def samp_prepK(s):
            if ("k", s) not in samp_state:
                samp_load_k(s)
            ckb, L_ckb = samp_state.pop(("k", s))
            tp, Bt = trb
            for half in range(2):
                fns = []
                for kt2 in range(2):
                    kt = half * 2 + kt2
                    for hc in range(4):
                        fns.append(lambda e, kt=kt, kt2=kt2, hc=hc: e.transpose(out=tp[:, (kt2 * 4 + hc) * 128:(kt2 * 4 + hc + 1) * 128],
                                                                                 in_=ckb[:, kt, hc * 128:(hc + 1) * 128], identity=ident[:]))
                k.grp(fns, reads=L_ckb + [B_c2], writes=[Bt])
                k.op(act, lambda e, half=half: e.activation(out=ckT[:, :, half * 256:(half + 1) * 256].rearrange("p c (k t) -> p k c t", k=2),
                                                            in_=tp[:].rearrange("p (k c t) -> p k c t", k=2, c=4), func=AF.Copy), reads=[Bt], writes=L_ckT)
            if s + 1 < NSEQ:
                samp_load_k(s + 1)
            samp_state[("kT", s)] = True

        def attn_sample(s):
            if ("kT", s) not in samp_state:
                samp_prepK(s)
            samp_state.pop(("kT", s))
            if ("v", s) not in samp_state:
                samp_load_v(s)
            vts = samp_state.pop(("v", s))
            tp, Bt = trb
            om, Bom = omr.get()
            scp = scpr.get(); scn = scpr.get()
            pts = [PTr.get(), PTr.get()]
            ptns = [PTnr.get(), PTnr.get()]
            for par in range(2):
                hb = par * 64
                psc, Bsc = scp[par]; psn, Bsn = scn[par]
                fns = []
                for kt in range(4):
                    for a in range(4):
                        fns.append(lambda e, kt=kt, a=a: e.matmul(psc[:, (kt * 4 + a) * TS:(kt * 4 + a + 1) * TS], lhsT=ckT[hb:hb + 64, a, kt * 128:(kt + 1) * 128],
                                                                  rhs=qnT[hb:hb + 64, a, s * TS:(s + 1) * TS], start=True, stop=True))
                for a in range(4):
                    fns.append(lambda e, a=a: e.matmul(psn[:, a * TS:(a + 1) * TS], lhsT=kT[hb:hb + 64, a, 0:128],
                                                       rhs=qnT[hb:hb + 64, a, s * TS:(s + 1) * TS], start=True, stop=True))
                k.grp(fns, reads=L_ckT + B_kT[0] + B_qnT, writes=[Bsc, Bsn])
            for par in range(2):
                psc, Bsc = scp[par]; psn, Bsn = scn[par]
                pt, Bpt = pts[par]; ptn, Bptn = ptns[par]
                k.op(act, lambda e: e.activation(out=pt[:], in_=psc[:], func=AF.Exp, scale=0.125), reads=[Bsc], writes=[Bpt])
                k.op(act, lambda e: e.activation(out=ptn[:], in_=psn[:, 0:128], func=AF.Exp, scale=0.125), reads=[Bsn], writes=[Bptn])
                eb5 = EB[:].rearrange("p (a b) m -> p a b m", b=2)
                for kt in range(4):
                    k.op(dve, lambda e, kt=kt: e.tensor_tensor(out=pt[:, kt * 128:(kt + 1) * 128].rearrange("p (h q) -> p h q", h=4),
                                                               in0=pt[:, kt * 128:(kt + 1) * 128].rearrange("p (h q) -> p h q", h=4),
                                                               in1=eb5[:, :, par, 512 - 128 * kt:512 - 128 * kt + TS], op=ALU.mult), reads=[Bpt, B_EB], writes=[Bpt])
                ebn5 = EBn[:].rearrange("p s (a b) q -> p s a b q", b=2)
                k.op(dve, lambda e: e.tensor_tensor(out=ptn[:].rearrange("p (h q) -> p h q", h=4), in0=ptn[:].rearrange("p (h q) -> p h q", h=4),
                                                    in1=ebn5[:, s, :, par, :], op=ALU.mult), reads=[Bptn, B_EB], writes=[Bptn])
                ppv, Bpv = pvb[par]
                fns = []
                for a in range(4):
                    h = 2 * a + par
                    for kt in range(4):
                        fns.append(lambda e, a=a, h=h, kt=kt: e.matmul(ppv[0:TS, a * 65:(a + 1) * 65], lhsT=pt[:, (kt * 4 + a) * TS:(kt * 4 + a + 1) * TS],
                                                                       rhs=vaug[:, vts[kt], h, :], start=(kt == 0), stop=False))
                    fns.append(lambda e, a=a, h=h: e.matmul(ppv[0:TS, a * 65:(a + 1) * 65], lhsT=ptn[:, a * TS:(a + 1) * TS], rhs=vaug[:, 0, h, :],
                                                            start=False, stop=True))
                k.grp(fns, reads=[Bpt, Bptn, B_v[0]] + [B_v[t_] for t_ in vts], writes=[Bpv])
            if s + 1 < NSEQ:
                samp_load_v(s + 1)
                samp_prepK(s + 1)
            for par in range(2):
                attn_epilogue(pvb[par][0], pvb[par][1], par, TS)
            k.op(dve, lambda e: e.tensor_tensor(out=om[0:TS, :], in0=o32[0:TS, :], in1=gass[:, s, :], op=ALU.mult), reads=[B_o32, B_gass[s]], writes=[Bom])
            k.grp([lambda e, c=c: e.transpose(out=tp[:, c * TS:(c + 1) * TS], in_=om[0:TS, c * 128:(c + 1) * 128], identity=ident[0:TS, 0:TS]) for c in range(4)],
                  reads=[Bom, B_c2], writes=[Bt])
            k.op(act, lambda e: e.activation(out=mixT[:, 4:8, s * TS:(s + 1) * TS], in_=tp[:, 0:4 * TS].rearrange("p (c t) -> p c t", c=4), func=AF.Copy),
                 reads=[Bt], writes=[B_mix[0]])

        def k_out(blk, jl):
            pt, Bp = mmr.get()
            k.grp([lambda e, c=c: e.transpose(out=pt[:, c * 128:(c + 1) * 128], in_=kn32l[c][0][:, jl * 128:(jl + 1) * 128], identity=identf[:]) for c in range(4)],
                  reads=B_kn32 + [B_c2], writes=[Bp])
            ko, Bko = koutr.get()
            k.op(dve, lambda e: e.tensor_copy(out=ko[:], in_=pt[:]), reads=[Bp], writes=[Bko])
            dst = pk[jl * 128:(jl + 1) * 128, :] if blk == 3 else sk
            k.dma(sp, dst, ko[:], Bko, reads=[Bko], is_out=True)

        phase2_state = {}

        back_pre = {}

        def back_load(tile):
            xt, Bx = xring.get()
            k.dma(sp, xt[:], x_src(tile), Bx, writes=[Bx])
            back_pre[tile] = (xt, Bx)

        def back_a(tile, jl):
            if tile not in back_pre:
                back_load(tile)
            xt, Bx = back_pre.pop(tile)
            pb, Bpb = pbr.get()
            k.dma(pq, pb[:], pp[tile * 128:(tile + 1) * 128, :] if tile < 16 else pss, Bpb, writes=[Bpb])
            for n in range(2):
                pt, Bp = mm2.get()
                k.grp([lambda e, kc=kc, n=n: e.matmul(pt[:], lhsT=mixT[:, kc, jl * 128:(jl + 1) * 128], rhs=w_out_sb[:, kc, n * 512:(n + 1) * 512],
                                                      start=(kc == 0), stop=(kc == 7)) for kc in range(8)], reads=[B_mix[jl], B_wout], writes=[Bp])
                k.op(dve, lambda e, n=n, pt=pt: e.tensor_tensor(out=xt[:, n * 512:(n + 1) * 512], in0=pt[:], in1=xt[:, n * 512:(n + 1) * 512], op=ALU.add),
                     reads=[Bp, Bx], writes=[Bx])
            hn, Bhn = hnr.get()
            k.op(act, lambda e: e.activation(out=hn[:], in_=xt[:], func=AF.Square, accum_out=pssq[:, tile:tile + 1]), reads=[Bx, B_c2], writes=[Bhn, B_pssq[tile]])
            k.op(act, lambda e: e.activation(out=prstd[:, tile:tile + 1], in_=pssq[:, tile:tile + 1], func=AF.Sqrt, scale=1.0 / D, bias=epst[:]),
                 reads=[B_pssq[tile]], writes=[B_pssq[tile]])
            k.op(dve, lambda e: e.reciprocal(out=prstd[:, tile:tile + 1], in_=prstd[:, tile:tile + 1]), reads=[B_pssq[tile]], writes=[B_pssq[tile]])
            k.op(dve, lambda e: e.scalar_tensor_tensor(out=hn[:], in0=xt[:], scalar=prstd[:, tile:tile + 1], in1=pg_bc[:], op0=ALU.mult, op1=ALU.mult),
                 reads=[Bx, B_pssq[tile], B_g2], writes=[Bhn])
            phase2_state[tile] = (xt, Bx, hn, Bhn, pb, Bpb)

        def back_a2(tile):
            xt, Bx, hn, Bhn, pb, Bpb = phase2_state.pop(tile)
            tp, Bt = trb
            k.grp([lambda e, c=c: e.transpose(out=tp[:, c * 128:(c + 1) * 128], in_=pb[:, c * 128:(c + 1) * 128], identity=ident[:]) for c in range(2)],
                  reads=[Bpb, B_c2], writes=[Bt])
            pT, BpT = pTr.get()
            k.op(act, lambda e: e.activation(out=pT[:], in_=tp[:, 0:256].rearrange("p (c t) -> p c t", c=2), func=AF.Copy), reads=[Bt], writes=[BpT])
            k.grp([lambda e, c=c: e.transpose(out=tp[:, c * 128:(c + 1) * 128], in_=hn[:, c * 128:(c + 1) * 128], identity=ident[:]) for c in range(8)],
                  reads=[Bhn, B_c2], writes=[Bt])
            hnT, BhT = hnTr.get()
            k.op(act, lambda e: e.activation(out=hnT[:], in_=tp[:].rearrange("p (c t) -> p c t", c=8), func=AF.Copy), reads=[Bt], writes=[BhT])
            phase2_state[tile] = (xt, Bx, hnT, BhT, pT, BpT)

        def back_b(tile):
            xt, Bx, hnT, BhT, pT, BpT = phase2_state.pop(tile)
            for n in range(2):
                pg_, Bg = mm2.get()
                k.grp([lambda e, kc=kc, n=n: e.matmul(pg_[:], lhsT=hnT[:, kc, :], rhs=w_pg_sb[:, kc, n * 512:(n + 1) * 512], start=(kc == 0), stop=(kc == 7))
                       for kc in range(8)], reads=[BhT, B_wpg], writes=[Bg])
                pe_, Be = mm2.get()
                k.grp([lambda e, kc=kc, n=n: e.matmul(pe_[:], lhsT=pT[:, kc, :], rhs=w_pe_sb[:, kc, n * 512:(n + 1) * 512], start=(kc == 0), stop=(kc == 1))
                       for kc in range(2)], reads=[BpT, B_wpe], writes=[Be])
                gt, Bgt = gater.get()
                k.op(act, lambda e: e.activation(out=gt[:], in_=pg_[:], func=AF.Sigmoid), reads=[Bg], writes=[Bgt])
                k.op(dve, lambda e: e.tensor_tensor(out=gt[:], in0=pe_[:], in1=gt[:], op=ALU.mult), reads=[Be, Bgt], writes=[Bgt])
                k.op(pool, lambda e, n=n: e.tensor_tensor(out=xt[:, n * 512:(n + 1) * 512], in0=xt[:, n * 512:(n + 1) * 512], in1=gt[:], op=ALU.add),
                     reads=[Bgt, Bx], writes=[Bx])
            dst = yp[tile * 128:(tile + 1) * 128, :] if tile < 16 else ys
            k.dma(sp, dst, xt[:], Bx, reads=[Bx], is_out=True)

        blocks = [(0, 512), (512, 512), (1024, 512), (1536, 512), (2048, 128)]
        for blk, (t0, N) in enumerate(blocks):
            if STAGE < 1 or (STAGE < 6 and blk > 0) or blk >= int(os.environ.get("MK_NBLK", "5")):
                break
            sample = blk == 4
            ntile = N // 128
            tile0 = t0 // 128
            front_a(tile0, 0)
            for jl in range(ntile):
                if jl + 1 < ntile:
                    front_a(tile0 + jl + 1, jl + 1)
                front_b(tile0 + jl, jl)
            if STAGE < 2:
                break
            prev = None
            for f in range(8, 16):
                cur = (f, qk1(f, N))
                if prev is not None:
                    qk2(prev[0], N, blk, prev[0] >= 12, prev[1])
                prev = cur
            qk2(prev[0], N, blk, True, prev[1])
            def vga(jl):
                vt = ((t0 // 128) + jl) % 8 if not sample else 0
                pt, Bp = mm_tm(jl, 2048, jl * 128, 128)
                k.op(act, lambda e, pt=pt, vt=vt: e.activation(out=vaug[:, vt, :, 0:64], in_=pt[:].rearrange("p (h d) -> p h d", h=8), func=AF.Copy),
                     reads=[Bp], writes=[B_v[vt]])
                if blk >= 3 and 'v' in FLG:
                    vo, Bvo = voutr.get()
                    k.op(dve, lambda e, pt=pt, vo=vo: e.tensor_copy(out=vo[:], in_=pt[:]), reads=[Bp, B_v[vt]], writes=[Bvo])
                    k.dma(sp, pv[jl * 128:(jl + 1) * 128, :] if blk == 3 else sv, vo[:], Bvo, reads=[Bvo], is_out=True)
                if not sample:
                    pt, Bp = mm_tm(jl, 2560, jl * 128, 128)
                    k.op(act, lambda e, pt=pt, jl=jl: e.activation(out=gas[:, jl, :], in_=pt[:], func=AF.Silu), reads=[Bp], writes=[B_gas[jl]])
                else:
                    for s in range(NSEQ):
                        pt, Bp = mm_tm(0, 2560, s * TS, TS)
                        k.op(act, lambda e, pt=pt, s=s: e.activation(out=gass[:, s, :], in_=pt[0:TS, :], func=AF.Silu), reads=[Bp], writes=[B_gass[s]])
            if blk >= 3 and 'k' in FLG:
                for jl in range(ntile):
                    k_out(blk, jl)

            def xlgl(c):
                if not sample and blk > 0:
                    k.op(pool, lambda e, c=c: e.tensor_copy(out=xl_ext[:, c, 0:3], in_=xl_ext[:, c, 512:515]), reads=[B_xl[c]], writes=[B_xl[c]])
                pt, Bp = mm_fm(c, N)
                if not sample:
                    k.op(act, lambda e, pt=pt, c=c: e.activation(out=xl_ext[:, c, 3:3 + N], in_=pt[:, 0:N], func=AF.Copy), reads=[Bp], writes=[B_xl[c]])
                    if blk == 3 and 'p' in FLG:
                        k.op(dve, lambda e, pt=pt, c=c: e.tensor_copy(out=pc32[:, c, :], in_=pt[:, N - 3:N]), reads=[Bp, B_xl[c]], writes=[B_pc])
                else:
                    k.op(act, lambda e, pt=pt, c=c: e.activation(out=xl_exs[:, c, :, 3:3 + TS], in_=pt[:, 0:N].rearrange("p (s t) -> p s t", s=NSEQ), func=AF.Copy),
                         reads=[Bp], writes=[B_xl[c]])
                    k.op(dve, lambda e, pt=pt, c=c: e.tensor_copy(out=sc32[:, c, :, :], in_=pt[:, 0:N].rearrange("p (s t) -> p s t", s=NSEQ)[:, :, TS - 3:TS]),
                         reads=[Bp, B_xl[c]], writes=[B_sc])
                pt, Bp = mm_fm(4 + c, N)
                k.op(act, lambda e, pt=pt, c=c: e.activation(out=gls[:, c, 0:N], in_=pt[:, 0:N], func=AF.Silu), reads=[Bp], writes=[B_gls[c]])
            if STAGE < 3:
                break
            xlgl(0); xlgl(1)
            if blk == 0:
                build_EB()
            ga_pair([0, 1], N, blk, to_gls=True)
            for jl in range(ntile):
                vga(jl)
            xlgl(2); xlgl(3)
            k.op(dve, lambda e: e.tensor_copy(out=mixT[:, 0:2, 0:N], in_=gls[:, 0:2, 0:N]), reads=[B_gls[0], B_gls[1]], writes=B_mix[0:ntile])
            ga_tail = ga_pair([2, 3], N, blk, defer=not sample)
            for jl in range(min(ntile, 3)):
                back_load(tile0 + jl)
            if not sample:
                attn_block(blk, ga_tail)
                if blk == 3:
                    samp_load_k(0)
                    samp_load_v(0)
            else:
                for s in range(NSEQ):
                    attn_sample(s)
            if STAGE < 5:
                break
            back_a(tile0, 0)
            for jl in range(ntile):
                if jl + 1 < ntile:
                    back_a(tile0 + jl + 1, jl + 1)
                back_a2(tile0 + jl)
                back_b(tile0 + jl)
                if jl + 3 < ntile:
                    back_load(tile0 + jl + 3)
                if ntile == 4 and blk + 1 < len(blocks) and jl in (1, 2):
                    nt = blocks[blk + 1][0] // 128 + (jl - 1)
                    if nt < blocks[blk + 1][0] // 128 + blocks[blk + 1][1] // 128:
                        preload[nt] = front_load(nt)

        if STAGE < 99:
            k.op(dve, lambda e: e.memset(pc32[:], 0.0), writes=[B_pc, B_ph, B_sc, B_sh])
        k.dma(sp, pc_t, pc32[:], B_pc, reads=[B_pc], is_out=True)
        k.dma(sp, ph_t, ph32[:], B_ph, reads=[B_ph], is_out=True)
        k.dma(sp, sc_t, sc32[:], B_sc, reads=[B_sc], is_out=True)
        k.dma(sp, sh_t, sh32[:], B_sh, reads=[B_sh], is_out=True)
        k.finish()
    return nc


_NC_CACHE = {}


def _prep_inputs(inp):
    f = lambda a: np.ascontiguousarray(a, dtype=np.float32)
    w_in = f(inp["w_in"][0]); w_out = f(inp["w_out"][0]); w_pg = f(inp["w_ple_gate"][0]); w_pe = f(inp["w_ple_proj"][0])
    ng = f(inp["norm_g"][0][None, :]); pg = f(inp["ple_norm_g"][0][None, :])
    cvs = np.stack([inp["conv_w"][0][0], inp["conv_w"][0][1], inp["conv_w"][0][2], inp["conv_w"][0][3], inp["conv_b"][0],
                    inp["gate_a_b"][0], inp["gate_x_b"][0], inp["lru_lambda"][0]])
    chanv = f(cvs.reshape(8, 4, 128).transpose(2, 0, 1))
    qkg = f(np.stack([np.tile(inp["q_norm_g"][0], 2), np.tile(inp["k_norm_g"][0], 2)], axis=1))

    def blockdiag(w):
        o = np.zeros((128, 4, 128), np.float32)
        for c in range(4):
            o[0:64, c, 0:64] = w[2 * c]
            o[64:128, c, 64:128] = w[2 * c + 1]
        return o
    wab = blockdiag(inp["gate_a_w"][0]); wxb = blockdiag(inp["gate_x_w"][0])
    relb = f(inp["rel_bias"][0].T)
    shared = dict(w_in=w_in, w_out=w_out, w_pg=w_pg, w_pe=w_pe, ng=ng, pg=pg, chanv=chanv, qkg=qkg, wab=wab, wxb=wxb, relb=relb)
    maps = []
    for c in range(NCORES):
        sl = slice(c * NSEQ, (c + 1) * NSEQ)
        m = dict(shared)
        m["xp"] = f(inp["x_prompt"][c]); m["xs"] = f(inp["x_sample"][sl].reshape(128, D))
        m["pp"] = f(inp["p_prompt"][0, c]); m["pss"] = f(inp["p_sample"][0, sl].reshape(128, 256))
        m["ck"] = f(inp["cache_k"][0, sl].reshape(NSEQ, 512, 512)); m["cv"] = f(inp["cache_v"][0, sl].reshape(NSEQ, 512, 512))
        m["sconv"] = f(inp["state_conv"][0, sl].reshape(NSEQ, 3, 4, 128).transpose(3, 2, 0, 1))
        m["slru"] = f(inp["state_lru"][0, sl].reshape(NSEQ, 4, 128).transpose(2, 1, 0))
        maps.append(m)
    return maps


def kernel(**inp):
    if "nc" not in _NC_CACHE:
        _NC_CACHE["nc"] = build_nc(int(os.environ.get("MK_STAGE", "99")))
    nc = _NC_CACHE["nc"]
    maps = _prep_inputs(inp)
    res = run_bass_kernel_spmd(nc, maps, core_ids=list(range(NCORES)))
    R = res.results
    yp = np.stack([R[c]["yp"] for c in range(NCORES)])
    ys = np.concatenate([R[c]["ys"].reshape(NSEQ, TS, D) for c in range(NCORES)])
    pk = np.stack([R[c]["pk"].reshape(512, 8, 64) for c in range(NCORES)])[None]
    pv = np.stack([R[c]["pv"].reshape(512, 8, 64) for c in range(NCORES)])[None]
    pc = np.stack([R[c]["pc_t"].transpose(2, 1, 0).reshape(3, 512) for c in range(NCORES)])[None]
    ph = np.stack([R[c]["ph_t"].transpose(1, 0).reshape(512) for c in range(NCORES)])[None]
    sk = np.concatenate([R[c]["sk"].reshape(NSEQ, TS, 8, 64) for c in range(NCORES)])[None]
    sv = np.concatenate([R[c]["sv"].reshape(NSEQ, TS, 8, 64) for c in range(NCORES)])[None]
    sc = np.concatenate([R[c]["sc_t"].transpose(2, 3, 1, 0).reshape(NSEQ, 3, 512) for c in range(NCORES)])[None]
    sh = np.concatenate([R[c]["sh_t"].transpose(2, 1, 0).reshape(NSEQ, 512) for c in range(NCORES)])[None]
    outs = (yp, ys, pk, pv, pc, ph, sk, sv, sc, sh)
    return tuple(np.ascontiguousarray(o, dtype=np.float32) for o in outs)
```

```python
import os
import numpy as np
import concourse.bass as bass
import concourse.mybir as mybir
from concourse.bass_utils import run_bass_kernel_spmd
from contextlib import ExitStack

F32 = mybir.dt.float32
BF16 = mybir.dt.bfloat16
AF = mybir.ActivationFunctionType
ALU = mybir.AluOpType

NCORES = 8
D = 1024
S = 2048
NSEQ = 4
TS = 32
EPS = 1e-6
WZ = 768


class Buf:
    __slots__ = ("w", "r", "dsem", "dcnt", "name")

    def __init__(self, name=""):
        self.w = None
        self.r = []
        self.dsem = {}
        self.dcnt = {}
        self.name = name


class Eng:
    def __init__(self, h, sem, kind):
        self.h = h
        self.sem = sem
        self.kind = kind
        self.cnt = 0
        self.waited = {}


class K:
    def __init__(self, nc, es):
        self.nc = nc
        self.es = es
        self.nsem = 0
        mk = lambda h, kind: Eng(h, self.sem(), kind)
        self.pe = mk(nc.tensor, "pe")
        self.act = mk(nc.scalar, "cmp")
        self.dve = mk(nc.vector, "cmp")
        self.pool = mk(nc.gpsimd, "cmp")
        self.sp = Eng(nc.sync, None, "dma")
        self.pq = Eng(nc.gpsimd, None, "dma")
        self.out_toks = []

    def sem(self):
        self.nsem += 1
        return self.es.enter_context(self.nc.semaphore("s%d" % self.nsem))

    def sb(self, name, shape, dt):
        return self.es.enter_context(self.nc.sbuf_tensor(name, shape, dt))

    def ps(self, name, shape, dt):
        return self.es.enter_context(self.nc.psum_tensor(name, shape, dt))

    def _deps(self, eng, reads, writes):
        deps = []
        for b in reads:
            if b.w is not None:
                deps.append(b.w)
        for b in writes:
            if b.w is not None:
                deps.append(b.w)
            deps.extend(b.r)
        for (sem, val) in deps:
            if sem is eng.sem:
                if eng.kind == "pe":
                    continue
            if eng.waited.get(id(sem), 0) >= val:
                continue
            eng.h.wait_ge(sem, val)
            eng.waited[id(sem)] = val

    def _mark(self, tok, reads, writes):
        for b in reads:
            b.r.append(tok)
        for b in writes:
            b.w = tok
            b.r = []

    def op(self, eng, fn, reads=(), writes=()):
        self._deps(eng, reads, writes)
        inst = fn(eng.h)
        eng.cnt += 1
        inst.then_inc(eng.sem, 1)
        tok = (eng.sem, eng.cnt)
        self._mark(tok, reads, writes)
        return tok

    def grp(self, fns, reads=(), writes=()):
        eng = self.pe
        self._deps(eng, reads, writes)
        inst = None
        for fn in fns:
            inst = fn(eng.h)
        eng.cnt += 1
        inst.then_inc(eng.sem, 1)
        tok = (eng.sem, eng.cnt)
        self._mark(tok, reads, writes)
        return tok

    def dma(self, q, out, in_, sbuf, reads=(), writes=(), is_out=False):
        self._deps(q, reads, writes)
        qk = id(q)
        if qk not in sbuf.dsem:
            sbuf.dsem[qk] = self.sem()
            sbuf.dcnt[qk] = 0
        inst = q.h.dma_start(out=out, in_=in_)
        sbuf.dcnt[qk] += 16
        inst.then_inc(sbuf.dsem[qk], 16)
        tok = (sbuf.dsem[qk], sbuf.dcnt[qk])
        self._mark(tok, reads, writes)
        if is_out:
            self.out_toks.append(tok)
        return tok

    def finish(self):
        last = {}
        for (sem, val) in self.out_toks:
            if last.get(id(sem), (None, 0))[1] < val:
                last[id(sem)] = (sem, val)
        for (sem, val) in last.values():
            self.sp.h.wait_ge(sem, val)


class Ring:
    def __init__(self, items):
        self.items = items
        self.i = 0

    def get(self):
        it = self.items[self.i % len(self.items)]
        self.i += 1
        return it


def build_nc(STAGE=99):
    FLG = os.environ.get('MK_FLG', 'kvp')
    SUB = os.environ.get('MK_SUB', 'z')
    nc = bass.Bass("TRN2", target_bir_lowering=False)
    di = lambda n, s: nc.dram_tensor(n, s, F32, kind="ExternalInput").ap()
    do = lambda n, s: nc.dram_tensor(n, s, F32, kind="ExternalOutput").ap()
    xp = di("xp", [S, D]); xs = di("xs", [128, D])
    pp = di("pp", [S, 256]); pss = di("pss", [128, 256])
    ck = di("ck", [NSEQ, 512, 512]); cv = di("cv", [NSEQ, 512, 512])
    sconv = di("sconv", [128, 4, NSEQ, 3]); slru = di("slru", [128, 4, NSEQ])
    w_in = di("w_in", [D, 3072]); w_out = di("w_out", [D, D]); w_pg = di("w_pg", [D, D]); w_pe = di("w_pe", [256, D])
    ng = di("ng", [1, D]); pg = di("pg", [1, D])
    chanv = di("chanv", [128, 8, 4]); qkg = di("qkg", [128, 2])
    wab = di("wab", [128, 4, 128]); wxb = di("wxb", [128, 4, 128])
    relb = di("relb", [8, 513])
    yp = do("yp", [S, D]); ys = do("ys", [128, D])
    pk = do("pk", [512, 512]); pv = do("pv", [512, 512])
    pc_t = do("pc_t", [128, 4, 3]); ph_t = do("ph_t", [128, 4])
    sk = do("sk", [128, 512]); sv = do("sv", [128, 512])
    sc_t = do("sc_t", [128, 4, NSEQ, 3]); sh_t = do("sh_t", [128, 4, NSEQ])
    zd = nc.dram_tensor("zd", [8, 128, WZ], F32, kind="Internal").ap()

    with ExitStack() as es:
        k = K(nc, es)
        pe, act, dve, pool, sp, pq = k.pe, k.act, k.dve, k.pool, k.sp, k.pq

        w_in_sb = k.sb("w_in_sb", [128, 8, 3072], BF16)
        w_out_sb = k.sb("w_out_sb", [128, 8, D], BF16)
        w_pg_sb = k.sb("w_pg_sb", [128, 8, D], BF16)
        w_pe_sb = k.sb("w_pe_sb", [128, 2, D], BF16)
        B_win = [Buf("win%d" % g) for g in range(24)]
        B_wout, B_wpg, B_wpe = Buf("wout"), Buf("wpg"), Buf("wpe")

        identf = k.sb("identf", [128, 128], F32); ident = k.sb("ident", [128, 128], BF16)
        onesb = k.sb("onesb", [128, 128], BF16)
        dconv = k.sb("dconv", [128, 4, 4, 128], BF16)
        wab_sb = k.sb("wab_sb", [128, 4, 128], BF16); wxb_sb = k.sb("wxb_sb", [128, 4, 128], BF16)
        ng_bc = k.sb("ng_bc", [128, D], F32); pg_bc = k.sb("pg_bc", [128, D], F32)
        chv = k.sb("chv", [128, 8, 4], F32)
        qkg_sb = k.sb("qkg_sb", [128, 2], F32)
        nsp = k.sb("nsp", [128, 8], F32)
        epst = k.sb("epst", [128, 1], F32)
        EB = k.sb("EB", [128, 8, 640], BF16)
        EBn = k.sb("EBn", [128, NSEQ, 8, TS], BF16)
        B_const = Buf("const")
        B_wg = Buf("wg")
        B_gbc = Buf("gbc")
        B_EB = Buf("EB")

        ssq = k.sb("ssq", [128, 17], F32); rstd = k.sb("rstd", [128, 17], F32)
        pssq = k.sb("pssq", [128, 17], F32); prstd = k.sb("prstd", [128, 17], F32)
        B_ssq = [Buf() for _ in range(17)]; B_pssq = [Buf() for _ in range(17)]

        NX = 3
        xpool = [(k.sb("xt%d" % i, [128, D], F32), Buf("xt%d" % i)) for i in range(NX)]
        xring = Ring(xpool)
        xnr = Ring([(k.sb("xn%d" % i, [128, D], BF16), Buf()) for i in range(2)])
        xl_ext = k.sb("xl_ext", [128, 4, 516], BF16); B_xl = [Buf() for _ in range(4)]
        xl_exs = k.sb("xl_exs", [128, 4, NSEQ, 36], BF16)
        gls = k.sb("gls", [128, 4, 512], BF16); B_gls = [Buf() for _ in range(4)]
        xc32r = Ring([(k.sb("xc32_%d" % i, [128, 512], F32), Buf()) for i in range(2)])
        xcbr = Ring([(k.sb("xcb%d" % i, [128, 512], BF16), Buf()) for i in range(2)])
        T1r = Ring([(k.sb("T1_%d" % i, [128, 512], F32), Buf()) for i in range(1)])
        T2r = Ring([(k.sb("T2_%d" % i, [128, 512], F32), Buf()) for i in range(2)])
        T3r = Ring([(k.sb("T3_%d" % i, [128, 512], F32), Buf()) for i in range(2)])
        T4r = Ring([(k.sb("T4_%d" % i, [128, 512], F32), Buf()) for i in range(2)])
        hsr = Ring([(k.sb("hs%d" % i, [128, 512], F32), Buf()) for i in range(1)])
        carry = k.sb("carry", [128, 4], F32); B_carry = [Buf() for _ in range(4)]
        qrawr = Ring([xc32r.items[0], xc32r.items[1]]); sqr = xcbr; rsr = T4r
        qnT = k.sb("qnT", [128, 4, 512], BF16); B_qnT = [Buf() for _ in range(4)]
        kT = k.sb("kT", [128, 4, 1024], BF16); B_kT = [[Buf() for _ in range(4)] for _ in range(2)]
        kn32l = [T2r.items[0], T2r.items[1], T3r.items[0], T3r.items[1]]
        B_kn32 = [b for (_, b) in kn32l]
        vaug = k.sb("vaug", [128, 8, 8, 65], BF16); B_v = [Buf() for _ in range(8)]
        gas = k.sb("gas", [128, 4, 512], BF16); B_gas = [Buf() for _ in range(4)]
        gass = gas[0:32, :, :]; B_gass = B_gas
        PTr = Ring([(k.sb("PT%d" % i, [128, 512], BF16), Buf()) for i in range(4)])
        PTnr = Ring([(k.sb("PTn%d" % i, [128, 128], BF16), Buf()) for i in range(2)])
        rden = k.sb("rden", [128, 8], F32); B_rden = Buf()
        omr = Ring([(k.sb("om%d" % i, [128, 512], BF16), Buf()) for i in range(2)])
        mixT = k.sb("mixT", [128, 8, 512], BF16); B_mix = [Buf() for _ in range(4)]
        xnT = mixT
        ckb = kT[:, :, 512:1024]; L_ckb = B_kT[1]
        ckT = gls; L_ckT = B_gls
        cvaug = vaug[:, 4:8, :, :]; L_cva = B_v[4:8]
        hnr = xnr
        hnTr = Ring([(k.sb("hnT%d" % i, [128, 8, 128], BF16), Buf()) for i in range(2)])
        gater = Ring([(k.sb("gate%d" % i, [128, 512], F32), Buf()) for i in range(2)])
        voutr = gater; koutr = gater
        o32, B_o32 = gater.items[1]
        pbr = Ring([(k.sb("pb%d" % i, [128, 256], BF16), Buf()) for i in range(2)])
        pTr = Ring([(k.sb("pT%d" % i, [128, 2, 128], BF16), Buf()) for i in range(1)])
        pc32 = k.sb("pc32", [128, 4, 3], F32); ph32 = k.sb("ph32", [128, 4], F32)
        sc32 = k.sb("sc32", [128, 4, NSEQ, 3], F32); sh32 = k.sb("sh32", [128, 4, NSEQ], F32)
        sconv_sb = k.sb("sconv_sb", [128, 4, NSEQ, 3], F32); slru_sb = k.sb("slru_sb", [128, 4, NSEQ], F32)
        B_pc, B_ph, B_sc, B_sh, B_sst = Buf(), Buf(), Buf(), Buf(), Buf()
        FsbA = gater.items[0][0][0:8, 0:384]; B_FA = gater.items[0][1]
        FsbB = gater.items[1][0][0:8, 0:384]; B_FB = gater.items[1][1]
        B_zd = Buf("zd")

        banks = [(k.ps("bank%d" % i, [128, 512], F32), Buf("bank%d" % i)) for i in range(7)]
        trb = (k.ps("ptr", [128, 1024], BF16), Buf("banktr"))
        mmr = Ring([banks[0], banks[1], banks[4], banks[5], banks[6], banks[2], banks[3]]); gabr = mmr; pvb = [banks[2], banks[3]]; scpr = Ring([(banks[4], banks[5]), (banks[6], banks[1])]); mm2 = Ring(banks[0:7])

        def x_src(tile):
            return xp[tile * 128:(tile + 1) * 128, :] if tile < 16 else xs

        def front_load(tile):
            xt, Bx = xring.get()
            k.dma(sp, xt[:], x_src(tile), Bx, writes=[Bx])
            return (xt, Bx)

        B_c2 = Buf("c2")
        k.op(pool, lambda e: e.memset(identf[:], 0.0), writes=[B_c2])
        k.op(pool, lambda e: e.affine_select(out=identf[:], in_=identf[:], pattern=[[-1, 128]], compare_op=ALU.not_equal,
                                              fill=1.0, base=0, channel_multiplier=1), writes=[B_c2])
        k.op(pool, lambda e: e.memset(onesb[:], 0.0), writes=[B_c2])
        k.op(pool, lambda e: e.memset(onesb[0:64, 0:64], 1.0 / 64), writes=[B_c2])
        k.op(pool, lambda e: e.memset(onesb[64:128, 64:128], 1.0 / 64), writes=[B_c2])
        k.op(pool, lambda e: e.memset(epst[:], EPS), writes=[B_c2])
        k.op(pool, lambda e: e.memset(ssq[:], 0.0), writes=[B_c2])
        k.op(pool, lambda e: e.memset(pssq[:], 0.0), writes=[B_c2])
        k.op(pool, lambda e: e.memset(carry[:], 0.0), writes=[B_c2] + B_carry)
        k.op(pool, lambda e: e.memset(xl_ext[:, :, 0:4], 0.0), writes=B_xl)
        k.op(pool, lambda e: e.memset(vaug[:, :, :, 64:65], 1.0), writes=B_v)
        w_in_v = w_in.rearrange("(k p) n -> p k n", p=128)
        for g in list(range(8, 24)) + list(range(0, 8)):
            k.dma(pq, w_in_sb[:, :, g * 128:(g + 1) * 128], w_in_v[:, :, g * 128:(g + 1) * 128], B_win[g], writes=[B_win[g]])
        B_c_list = [Buf() for _ in range(4)]
        k.dma(sp, chv[:], chanv, B_c_list[0], writes=[B_c_list[0]])
        k.dma(sp, qkg_sb[:], qkg, B_c_list[1], writes=[B_c_list[1]])
        k.dma(sp, sconv_sb[:], sconv, B_c_list[2], writes=[B_c_list[2]])
        k.dma(sp, slru_sb[:], slru, B_c_list[3], writes=[B_c_list[3]])
        B_g1, B_g2 = Buf(), Buf()
        k.dma(sp, ng_bc[:], ng.to_broadcast([128, D]), B_g1, writes=[B_g1])
        k.dma(sp, pg_bc[:], pg.to_broadcast([128, D]), B_g2, writes=[B_g2])
        preload = {t: front_load(t) for t in range(3)}
        k.dma(sp, FsbA, relb[:, 129:513], B_FA, writes=[B_FA])
        k.op(dve, lambda e: e.tensor_copy(out=FsbB, in_=FsbA[:, 383:384].to_broadcast([8, 384])), reads=[B_FA], writes=[B_FB])
        k.dma(sp, zd[:, 0:8, 0:384], FsbA.unsqueeze(1).to_broadcast([8, 8, 384]), B_zd, reads=[B_FA], writes=[B_zd])
        k.dma(sp, zd[:, 0:8, 384:WZ], FsbB.unsqueeze(1).to_broadcast([8, 8, 384]), B_zd, reads=[B_FB], writes=[B_zd])
        nrow = 8
        while nrow < 128:
            k.dma(sp, zd[:, nrow:2 * nrow, :], zd[:, 0:nrow, :], B_zd, reads=[B_zd], writes=[B_zd])
            nrow *= 2
        k.op(dve, lambda e: e.memset(rden[:, 0:1], 0.0), reads=B_c_list, writes=[B_const])
        B_wg2 = Buf()
        k.dma(pq, wab_sb[:], wab, B_wg, writes=[B_wg])
        k.dma(pq, wxb_sb[:], wxb, B_wg2, writes=[B_wg2])
        k.dma(pq, w_out_sb[:], w_out.rearrange("(k p) n -> p k n", p=128), B_wout, writes=[B_wout])
        k.dma(pq, w_pg_sb[:], w_pg.rearrange("(k p) n -> p k n", p=128), B_wpg, writes=[B_wpg])
        k.dma(pq, w_pe_sb[:], w_pe.rearrange("(k p) n -> p k n", p=128), B_wpe, writes=[B_wpe])

        k.op(dve, lambda e: e.tensor_copy(out=ident[:], in_=identf[:]), reads=[B_c2], writes=[B_c2])
        for t in range(4):
            for c in range(4):
                k.op(dve, lambda e, t=t, c=c: e.tensor_scalar(out=dconv[:, t, c, :], in0=identf[:], scalar1=chv[:, t, c:c + 1],
                                                               scalar2=None, op0=ALU.mult), reads=[B_const, B_c2], writes=[B_c2])
        k.op(dve, lambda e: e.tensor_copy(out=xl_exs[:, :, :, 0:3], in_=sconv_sb[:]), reads=[B_const], writes=B_xl)
        k.op(act, lambda e: e.activation(out=nsp[:, 0:4], in_=chv[:, 7, :], func=AF.Exp, scale=-1.0), reads=[B_const], writes=[B_c2])
        k.op(act, lambda e: e.activation(out=nsp[:, 0:4], in_=nsp[:, 0:4], func=AF.Ln, bias=1.0), reads=[B_c2], writes=[B_c2])
        k.op(dve, lambda e: e.tensor_scalar(out=nsp[:, 4:8], in0=nsp[:, 0:4], scalar1=-16.0, scalar2=None, op0=ALU.mult), reads=[B_c2], writes=[B_c2])
        k.op(dve, lambda e: e.tensor_scalar(out=nsp[:, 0:4], in0=nsp[:, 0:4], scalar1=-8.0, scalar2=None, op0=ALU.mult), reads=[B_c2], writes=[B_c2])
        def build_EB():
            stg = [gater.items[0], gater.items[1], T2r.items[0], T2r.items[1]]
            for h in range(8):
                for half in range(2):
                    t, B = stg[(2 * h + half) % 4]
                    src = bass.AP(zd.tensor, h * 128 * WZ + 127 + half * 320, [[WZ - 1, 128], [1, 320]])
                    k.dma(sp, t[:, 0:320], src, B, reads=[B_zd], writes=[B])
                    k.op(act, lambda e, h=h, half=half, t=t: e.activation(out=EB[:, h, half * 320:(half + 1) * 320], in_=t[:, 0:320], func=AF.Exp),
                         reads=[B], writes=[B_EB])
            for g in range(2):
                t, B = T3r.items[g]
                tv = t[:].rearrange("p (s h q) -> p s h q", s=2, h=8)
                k.op(dve, lambda e, t=t: e.memset(t[:], -30000.0), writes=[B])
                for s2 in range(2):
                    s_ = 2 * g + s2
                    src = bass.AP(zd.tensor, 127, [[WZ - 1, TS], [128 * WZ, 8], [1, TS]])
                    k.dma(sp, tv[s_ * TS:(s_ + 1) * TS, s2, :, :], src, B, reads=[B_zd], writes=[B])
                k.op(act, lambda e, g=g, tv=tv: e.activation(out=EBn[:, 2 * g:2 * g + 2, :, :], in_=tv, func=AF.Exp), reads=[B], writes=[B_EB])
            k.op(pool, lambda e: e.memset(EB[0:64, :, 512 + 64:640], 0.0), reads=[], writes=[B_EB])
            k.op(pool, lambda e: e.memset(EB[64:128, :, 0:64], 0.0), reads=[], writes=[B_EB])

        fr_state = {}

        def front_a(tile, jl):
            xt, Bx = preload.pop(tile) if tile in preload else front_load(tile)
            xn, Bxn = xnr.get()
            fr_state[tile] = (xn, Bxn)
            k.op(act, lambda e: e.activation(out=xn[:], in_=xt[:], func=AF.Square, accum_out=ssq[:, tile:tile + 1]),
                 reads=[Bx, B_c2], writes=[Bxn, B_ssq[tile]])
            k.op(act, lambda e: e.activation(out=rstd[:, tile:tile + 1], in_=ssq[:, tile:tile + 1], func=AF.Ln, scale=1.0 / D, bias=epst[:]),
                 reads=[B_ssq[tile]], writes=[B_ssq[tile]])
            k.op(act, lambda e: e.activation(out=rstd[:, tile:tile + 1], in_=rstd[:, tile:tile + 1], func=AF.Exp, scale=-0.5),
                 reads=[B_ssq[tile]], writes=[B_ssq[tile]])
            k.op(dve, lambda e: e.scalar_tensor_tensor(out=xn[:], in0=xt[:], scalar=rstd[:, tile:tile + 1], in1=ng_bc[:], op0=ALU.mult, op1=ALU.mult),
                 reads=[Bx, B_ssq[tile], B_g1], writes=[Bxn])

        def front_b(tile, jl):
            xn, Bxn = fr_state.pop(tile)
            tp, Bt = trb
            k.grp([lambda e, c=c: e.transpose(out=tp[:, c * 128:(c + 1) * 128], in_=xn[:, c * 128:(c + 1) * 128], identity=ident[:]) for c in range(8)],
                  reads=[Bxn, B_c2], writes=[Bt])
            k.op(act, lambda e: e.activation(out=xnT[:, :, jl * 128:(jl + 1) * 128], in_=tp[:].rearrange("p (c t) -> p c t", c=8), func=AF.Copy),
                 reads=[Bt], writes=[B_mix[jl]])

        def mm_fm(f, N):
            pt, Bp = mmr.get()
            k.grp([lambda e, kc=kc: e.matmul(pt[:, 0:N], lhsT=w_in_sb[:, kc, f * 128:(f + 1) * 128], rhs=xnT[:, kc, 0:N],
                                             start=(kc == 0), stop=(kc == 7)) for kc in range(8)],
                  reads=B_mix + [B_win[f]], writes=[Bp])
            return pt, Bp

        def qk1(f, N):
            pt, Bp = mm_fm(f, N)
            sq, Bs = sqr.get()
            k.op(act, lambda e: e.activation(out=sq[:, 0:N], in_=pt[:, 0:N], func=AF.Square), reads=[Bp], writes=[Bs])
            return (pt, Bp, sq, Bs)

        def qk2(f, N, blk, is_k, st):
            c = f % 4
            qr, Bq, sq, Bs = st
            rs, Br = rsr.get()
            p2, Bp2 = mmr.get()
            k.grp([lambda e: e.matmul(p2[:, 0:N], lhsT=onesb[:], rhs=sq[:, 0:N], start=True, stop=True)], reads=[Bs, B_c2], writes=[Bp2])
            k.op(act, lambda e: e.activation(out=rs[:, 0:N], in_=p2[:, 0:N], func=AF.Ln, bias=epst[:]), reads=[Bp2, B_c2], writes=[Br])
            k.op(act, lambda e: e.activation(out=rs[:, 0:N], in_=rs[:, 0:N], func=AF.Exp, scale=-0.5), reads=[Br], writes=[Br])
            gcol = qkg_sb[:, 1:2] if is_k else qkg_sb[:, 0:1]
            if not is_k:
                k.op(dve, lambda e: e.scalar_tensor_tensor(out=qnT[:, c, 0:N], in0=qr[:, 0:N], scalar=gcol, in1=rs[:, 0:N], op0=ALU.mult, op1=ALU.mult),
                     reads=[Bq, Br, B_const], writes=[B_qnT[c]])
            else:
                slot = blk % 2
                if blk >= 3 and 'k' in FLG:
                    kn = kn32l[c][0]
                    k.op(dve, lambda e: e.scalar_tensor_tensor(out=kn[:, 0:N], in0=qr[:, 0:N], scalar=gcol, in1=rs[:, 0:N], op0=ALU.mult, op1=ALU.mult),
                         reads=[Bq, Br, B_const], writes=[B_kn32[c]])
                    k.op(pool, lambda e: e.tensor_copy(out=kT[:, c, slot * 512:slot * 512 + N], in_=kn[:, 0:N]), reads=[B_kn32[c]], writes=[B_kT[slot][c]])
                else:
                    k.op(dve, lambda e: e.scalar_tensor_tensor(out=kT[:, c, slot * 512:slot * 512 + N], in0=qr[:, 0:N], scalar=gcol, in1=rs[:, 0:N],
                                                               op0=ALU.mult, op1=ALU.mult), reads=[Bq, Br, B_const], writes=[B_kT[slot][c]])

        def mm_tm(jl, col0, M0, M):
            pt, Bp = mmr.get()
            g = col0 // 128
            k.grp([lambda e, kc=kc: e.matmul(pt[0:M, :], lhsT=xnT[:, kc, M0:M0 + M], rhs=w_in_sb[:, kc, col0:col0 + 512],
                                             start=(kc == 0), stop=(kc == 7)) for kc in range(8)],
                  reads=B_mix + B_win[g:g + 4], writes=[Bp])
            return pt, Bp

        ga_state = {}

        def group_a(c, N, blk):
            ga1(c, N, blk)
            ga2(c, N, blk)

        def ga1(c, N, blk):
            sample = blk == 4
            pt, Bp = mmr.get()
            if not sample:
                k.grp([lambda e, t=t: e.matmul(pt[:, 0:N], lhsT=dconv[:, t, c, :], rhs=xl_ext[:, c, t:t + N], start=(t == 0), stop=(t == 3)) for t in range(4)],
                      reads=[B_xl[c], B_c2], writes=[Bp])
            else:
                fns = []
                for s in range(NSEQ):
                    for t in range(4):
                        fns.append(lambda e, t=t, s=s: e.matmul(pt[:, s * TS:(s + 1) * TS], lhsT=dconv[:, t, c, :], rhs=xl_exs[:, c, s, t:t + TS],
                                                                 start=(t == 0), stop=(t == 3)))
                k.grp(fns, reads=[B_xl[c], B_c2], writes=[Bp])
            xc, Bxc = xc32r.get(); xb, Bxb = xcbr.get()
            k.op(act, lambda e: e.activation(out=xc[:, 0:N], in_=pt[:, 0:N], func=AF.Identity, bias=chv[:, 4, c:c + 1]), reads=[Bp, B_const], writes=[Bxc])
            k.op(pool, lambda e: e.tensor_copy(out=xb[:, 0:N], in_=xc[:, 0:N]), reads=[Bxc], writes=[Bxb])
            ga_state[c] = (xc, Bxc, xb, Bxb)

        def ga2(c, N, blk):
            sample = blk == 4
            xc, Bxc, xb, Bxb = ga_state.pop(c)
            pr, Bpr = mmr.get()
            k.grp([lambda e: e.matmul(pr[:, 0:N], lhsT=wab_sb[:, c, :], rhs=xb[:, 0:N], start=True, stop=True)], reads=[Bxb, B_wg], writes=[Bpr])
            pi, Bpi = mmr.get()
            k.grp([lambda e: e.matmul(pi[:, 0:N], lhsT=wxb_sb[:, c, :], rhs=xb[:, 0:N], start=True, stop=True)], reads=[Bxb, B_wg2], writes=[Bpi])
            t1, B1 = T1r.get(); t2, B2 = T2r.get(); t3, B3 = T3r.get(); t4, B4 = T4r.get()
            k.op(act, lambda e: e.activation(out=t1[:, 0:N], in_=pr[:, 0:N], func=AF.Sigmoid, bias=chv[:, 5, c:c + 1]), reads=[Bpr, B_const], writes=[B1])
            k.op(act, lambda e: e.activation(out=t2[:, 0:N], in_=pi[:, 0:N], func=AF.Sigmoid, bias=chv[:, 6, c:c + 1]), reads=[Bpi, B_const], writes=[B2])
            k.op(act, lambda e: e.activation(out=t3[:, 0:N], in_=t1[:, 0:N], func=AF.Exp, scale=nsp[:, c:c + 1]), reads=[B1, B_c2], writes=[B3])
            k.op(act, lambda e: e.activation(out=t4[:, 0:N], in_=t1[:, 0:N], func=AF.Exp, scale=nsp[:, 4 + c:5 + c]), reads=[B1, B_c2], writes=[B4])
            k.op(act, lambda e: e.activation(out=t4[:, 0:N], in_=t4[:, 0:N], func=AF.Sqrt, scale=-1.0, bias=1.0), reads=[B4], writes=[B4])
            k.op(dve, lambda e: e.tensor_tensor(out=t2[:, 0:N], in0=t2[:, 0:N], in1=xc[:, 0:N], op=ALU.mult), reads=[B2, Bxc], writes=[B2])
            k.op(dve, lambda e: e.tensor_tensor(out=t2[:, 0:N], in0=t2[:, 0:N], in1=t4[:, 0:N], op=ALU.mult), reads=[B2, B4], writes=[B2])
            hs, Bh = hsr.get()
            if not sample:
                k.op(dve, lambda e: e.tensor_tensor_scan(out=hs[:, 0:N], data0=t3[:, 0:N], data1=t2[:, 0:N], initial=carry[:, c:c + 1], op0=ALU.mult, op1=ALU.add),
                     reads=[B3, B2, B_carry[c]], writes=[Bh])
                k.op(dve, lambda e: e.tensor_copy(out=carry[:, c:c + 1], in_=hs[:, N - 1:N]), reads=[Bh], writes=[B_carry[c]])
                if blk == 3 and 'p' in FLG:
                    k.op(dve, lambda e: e.tensor_copy(out=ph32[:, c:c + 1], in_=hs[:, N - 1:N]), reads=[Bh], writes=[B_ph])
            else:
                for s in range(NSEQ):
                    k.op(dve, lambda e, s=s: e.tensor_tensor_scan(out=hs[:, s * TS:(s + 1) * TS], data0=t3[:, s * TS:(s + 1) * TS], data1=t2[:, s * TS:(s + 1) * TS],
                                                                  initial=slru_sb[:, c, s:s + 1], op0=ALU.mult, op1=ALU.add),
                         reads=[B3, B2, B_const], writes=[Bh])
                k.op(dve, lambda e: e.tensor_copy(out=sh32[:, c, :], in_=hs[:, 0:N].rearrange("p (s t) -> p s t", s=NSEQ)[:, :, TS - 1]), reads=[Bh], writes=[B_sh])
            ntile = N // 128
            k.op(dve, lambda e: e.tensor_tensor(out=mixT[:, c, 0:N], in0=hs[:, 0:N], in1=gls[:, c, 0:N], op=ALU.mult),
                 reads=[Bh, B_gls[c]], writes=B_mix[0:ntile])

        def ga_pair(cs, N, blk, defer=False, to_gls=False):
            sample = blk == 4
            tail = []

            def T(eng, fn, reads=(), writes=()):
                if defer:
                    tail.append(lambda: k.op(eng, fn, reads=reads, writes=writes))
                else:
                    k.op(eng, fn, reads=reads, writes=writes)
            st = {}
            for c in cs:
                pt, Bp = gabr.get()
                if not sample:
                    k.grp([lambda e, t=t: e.matmul(pt[:, 0:N], lhsT=dconv[:, t, c, :], rhs=xl_ext[:, c, t:t + N], start=(t == 0), stop=(t == 3)) for t in range(4)],
                          reads=[B_xl[c], B_c2], writes=[Bp])
                else:
                    fns = []
                    for s_ in range(NSEQ):
                        for t in range(4):
                            fns.append(lambda e, t=t, s_=s_: e.matmul(pt[:, s_ * TS:(s_ + 1) * TS], lhsT=dconv[:, t, c, :], rhs=xl_exs[:, c, s_, t:t + TS],
                                                                       start=(t == 0), stop=(t == 3)))
                    k.grp(fns, reads=[B_xl[c], B_c2], writes=[Bp])
                st[c] = [pt, Bp]
            for c in cs:
                pt, Bp = st[c]
                xc, Bxc = xc32r.get(); xb, Bxb = xcbr.get()
                k.op(act, lambda e: e.activation(out=xc[:, 0:N], in_=pt[:, 0:N], func=AF.Identity, bias=chv[:, 4, c:c + 1]), reads=[Bp, B_const], writes=[Bxc])
                k.op(dve, lambda e: e.tensor_copy(out=xb[:, 0:N], in_=xc[:, 0:N]), reads=[Bxc], writes=[Bxb])
                st[c] = [xc, Bxc, xb, Bxb]
            t1s = [T1r.items[0], gater.items[0]]
            for i, c in enumerate(cs):
                xc, Bxc, xb, Bxb = st[c]
                pr, Bpr = gabr.get()
                k.grp([lambda e: e.matmul(pr[:, 0:N], lhsT=wab_sb[:, c, :], rhs=xb[:, 0:N], start=True, stop=True)], reads=[Bxb, B_wg], writes=[Bpr])
                pi, Bpi = gabr.get()
                k.grp([lambda e: e.matmul(pi[:, 0:N], lhsT=wxb_sb[:, c, :], rhs=xb[:, 0:N], start=True, stop=True)], reads=[Bxb, B_wg2], writes=[Bpi])
                st[c] += [pr, Bpr, pi, Bpi, t1s[i], T2r.get(), T3r.get(), T4r.get()]
            for c in cs:
                xc, Bxc, xb, Bxb, pr, Bpr, pi, Bpi, (t1, B1), (t2, B2), (t3, B3), (t4, B4) = st[c]
                k.op(act, lambda e: e.activation(out=t1[:, 0:N], in_=pr[:, 0:N], func=AF.Sigmoid, bias=chv[:, 5, c:c + 1]), reads=[Bpr, B_const], writes=[B1])
                k.op(act, lambda e: e.activation(out=t2[:, 0:N], in_=pi[:, 0:N], func=AF.Sigmoid, bias=chv[:, 6, c:c + 1]), reads=[Bpi, B_const], writes=[B2])
            for c in cs:
                xc, Bxc, xb, Bxb, pr, Bpr, pi, Bpi, (t1, B1), (t2, B2), (t3, B3), (t4, B4) = st[c]
                T(act, lambda e, t1=t1, t3=t3, c=c: e.activation(out=t3[:, 0:N], in_=t1[:, 0:N], func=AF.Exp, scale=nsp[:, c:c + 1]), reads=[B1, B_c2], writes=[B3])
                T(act, lambda e, t1=t1, t4=t4, c=c: e.activation(out=t4[:, 0:N], in_=t1[:, 0:N], func=AF.Exp, scale=nsp[:, 4 + c:5 + c]), reads=[B1, B_c2], writes=[B4])
            for c in cs:
                xc, Bxc, xb, Bxb, pr, Bpr, pi, Bpi, (t1, B1), (t2, B2), (t3, B3), (t4, B4) = st[c]
                T(act, lambda e, t4=t4: e.activation(out=t4[:, 0:N], in_=t4[:, 0:N], func=AF.Sqrt, scale=-1.0, bias=1.0), reads=[B4], writes=[B4])
            hs, Bh = hsr.get()
            for c in cs:
                xc, Bxc, xb, Bxb, pr, Bpr, pi, Bpi, (t1, B1), (t2, B2), (t3, B3), (t4, B4) = st[c]
                T(dve, lambda e, t2=t2, xc=xc: e.tensor_tensor(out=t2[:, 0:N], in0=t2[:, 0:N], in1=xc[:, 0:N], op=ALU.mult), reads=[B2, Bxc], writes=[B2])
                T(dve, lambda e, t2=t2, t4=t4: e.tensor_tensor(out=t2[:, 0:N], in0=t2[:, 0:N], in1=t4[:, 0:N], op=ALU.mult), reads=[B2, B4], writes=[B2])
                if not sample:
                    T(dve, lambda e, t2=t2, t3=t3, c=c: e.tensor_tensor_scan(out=hs[:, 0:N], data0=t3[:, 0:N], data1=t2[:, 0:N], initial=carry[:, c:c + 1],
                                                                             op0=ALU.mult, op1=ALU.add), reads=[B3, B2, B_carry[c]], writes=[Bh])
                    T(dve, lambda e, c=c: e.tensor_copy(out=carry[:, c:c + 1], in_=hs[:, N - 1:N]), reads=[Bh], writes=[B_carry[c]])
                    if blk == 3 and 'p' in FLG:
                        T(dve, lambda e, c=c: e.tensor_copy(out=ph32[:, c:c + 1], in_=hs[:, N - 1:N]), reads=[Bh], writes=[B_ph])
                else:
                    for s_ in range(NSEQ):
                        T(dve, lambda e, s_=s_, t2=t2, t3=t3, c=c: e.tensor_tensor_scan(out=hs[:, s_ * TS:(s_ + 1) * TS], data0=t3[:, s_ * TS:(s_ + 1) * TS],
                                                                                       data1=t2[:, s_ * TS:(s_ + 1) * TS], initial=slru_sb[:, c, s_:s_ + 1],
                                                                                       op0=ALU.mult, op1=ALU.add), reads=[B3, B2, B_const], writes=[Bh])
                    T(dve, lambda e, c=c: e.tensor_copy(out=sh32[:, c, :], in_=hs[:, 0:N].rearrange("p (s t) -> p s t", s=NSEQ)[:, :, TS - 1]), reads=[Bh], writes=[B_sh])
                if to_gls:
                    T(dve, lambda e, c=c: e.tensor_tensor(out=gls[:, c, 0:N], in0=hs[:, 0:N], in1=gls[:, c, 0:N], op=ALU.mult),
                      reads=[Bh, B_gls[c]], writes=[B_gls[c]])
                else:
                    T(dve, lambda e, c=c: e.tensor_tensor(out=mixT[:, c, 0:N], in0=hs[:, 0:N], in1=gls[:, c, 0:N], op=ALU.mult),
                      reads=[Bh, B_gls[c]], writes=B_mix[0:N // 128])
            return tail

        def attn_epilogue(ppv, Bpv, par, rows):
            pvv = ppv[0:rows, 0:260].rearrange("p (h d) -> p h d", h=4)
            o4 = o32[0:rows, :].rearrange("p (a b d) -> p a b d", a=4, b=2)
            k.op(dve, lambda e: e.reciprocal(out=rden[0:rows, par * 4:par * 4 + 4], in_=pvv[:, :, 64]), reads=[Bpv], writes=[B_rden])
            k.op(dve, lambda e: e.tensor_tensor(out=o4[:, :, par, :], in0=pvv[:, :, 0:64],
                                                in1=rden[0:rows, par * 4:par * 4 + 4].unsqueeze(2).to_broadcast([rows, 4, 64]), op=ALU.mult),
                 reads=[Bpv, B_rden], writes=[B_o32])

        def attn_block(blk, tail):
            items = []
            for p in range(4):
                P = 4 * blk + p
                kts = [kt for kt in range(5) if P - 4 + kt >= 0]
                for kt in kts:
                    items.append((p, kt, kt == kts[0], kt == kts[-1]))
            st = {}
            oms = {}

            def QK(i):
                p, kt, first, last = items[i]
                P = 4 * blk + p
                g0 = 128 * (P - 4 + kt)
                kb = g0 // 512; slot = kb % 2; col = slot * 512 + g0 % 512
                vt = (g0 // 128) % 8
                scp = scpr.get()
                pts = [PTr.get(), PTr.get()]
                st[i] = (scp, pts, vt)
                for par in range(2):
                    psc, Bsc = scp[par]
                    hb = par * 64
                    k.grp([lambda e, a=a: e.matmul(psc[:, a * 128:(a + 1) * 128], lhsT=kT[hb:hb + 64, a, col:col + 128],
                                                   rhs=qnT[hb:hb + 64, a, p * 128:(p + 1) * 128], start=True, stop=True) for a in range(4)],
                          reads=B_kT[slot] + B_qnT, writes=[Bsc])

            def EXP(i):
                p, kt, first, last = items[i]
                scp, pts, vt = st[i]
                for par in range(2):
                    psc, Bsc = scp[par]
                    pt, Bpt = pts[par]
                    k.op(act, lambda e: e.activation(out=pt[:], in_=psc[:], func=AF.Exp, scale=0.125), reads=[Bsc], writes=[Bpt])
                    ebv = EB[:].rearrange("p (a b) m -> p a b m", b=2)[:, :, par, 512 - 128 * kt:640 - 128 * kt]
                    k.op(dve, lambda e: e.tensor_tensor(out=pt[:].rearrange("p (h q) -> p h q", h=4), in0=pt[:].rearrange("p (h q) -> p h q", h=4),
                                                        in1=ebv, op=ALU.mult), reads=[Bpt, B_EB], writes=[Bpt])

            def PV(i):
                p, kt, first, last = items[i]
                scp, pts, vt = st.pop(i)
                for par in range(2):
                    pt, Bpt = pts[par]
                    ppv, Bpv = pvb[par]
                    k.grp([lambda e, a=a: e.matmul(ppv[:, a * 65:(a + 1) * 65], lhsT=pt[:, a * 128:(a + 1) * 128], rhs=vaug[:, vt, 2 * a + par, :],
                                                   start=(first and a == 0), stop=last, skip_group_check=True) for a in range(4)],
                          reads=[Bpt, B_v[vt]], writes=[Bpv])

            def EPI(p):
                om, Bom = omr.get()
                oms[p] = (om, Bom)
                for par in range(2):
                    attn_epilogue(pvb[par][0], pvb[par][1], par, 128)
                k.op(dve, lambda e: e.tensor_tensor(out=om[:], in0=o32[:], in1=gas[:, p, :], op=ALU.mult), reads=[B_o32, B_gas[p]], writes=[Bom])

            def TR(p):
                om, Bom = oms.pop(p)
                tp, Bt = trb
                k.grp([lambda e, c=c: e.transpose(out=tp[:, c * 128:(c + 1) * 128], in_=om[:, c * 128:(c + 1) * 128], identity=ident[:]) for c in range(4)],
                      reads=[Bom, B_c2], writes=[Bt])
                k.op(act, lambda e: e.activation(out=mixT[:, 4:8, p * 128:(p + 1) * 128], in_=tp[:, 0:512].rearrange("p (c t) -> p c t", c=4), func=AF.Copy),
                     reads=[Bt], writes=[B_mix[p]])

            n = len(items)
            QK(0)
            pending = None
            for i in range(n):
                if i + 1 < n:
                    QK(i + 1)
                EXP(i)
                PV(i)
                if pending is not None:
                    TR(pending)
                    pending = None
                if items[i][3]:
                    EPI(items[i][0])
                    pending = items[i][0]
                for _ in range(2):
                    if tail:
                        tail.pop(0)()
            if pending is not None:
                TR(pending)
            while tail:
                tail.pop(0)()

        samp_state = {}

        def samp_load_k(s):
            if s % 2 == 0:
                ckb_s, L = kT[:, :, 512:1024], B_kT[1]
            else:
                ckb_s, L = xl_ext[:, :, 0:512], B_xl
            k.dma(pq, ckb_s, ck[s].rearrange("(t p) f -> p t f", p=128), L[0], writes=L)
            samp_state[("k", s)] = (ckb_s, L)

        def samp_load_v(s):
            vts = []
            for kt in range(4):
                ti = 1 + (4 * s + kt) % 7
                k.dma(pq, vaug[:, ti, :, 0:64], cv[s][kt * 128:(kt + 1) * 128, :].rearrange("p (h d) -> p h d", h=8), B_v[ti], writes=[B_v[ti]])
                vts.append(ti)
            samp_state[("v", s)] = vts

        def samp_prepK(s):
            if ("k", s) not in samp_state:
                samp_load_k(s)
            ckb, L_ckb = samp_state.pop(("k", s))
            tp, Bt = trb
            for half in range(2):
                fns = []
                for kt2 in range(2):
                    kt = half * 2 + kt2
                    for hc in range(4):
                        fns.append(lambda e, kt=kt, kt2=kt2, hc=hc: e.transpose(out=tp[:, (kt2 * 4 + hc) * 128:(kt2 * 4 + hc + 1) * 128],
                                                                                 in_=ckb[:, kt, hc * 128:(hc + 1) * 128], identity=ident[:]))
                k.grp(fns, reads=L_ckb + [B_c2], writes=[Bt])
                k.op(act, lambda e, half=half: e.activation(out=ckT[:, :, half * 256:(half + 1) * 256].rearrange("p c (k t) -> p k c t", k=2),
                                                            in_=tp[:].rearrange("p (k c t) -> p k c t", k=2, c=4), func=AF.Copy), reads=[Bt], writes=L_ckT)
            if s + 1 < NSEQ:
                samp_load_k(s + 1)
            samp_state[("kT", s)] = True

        def attn_sample(s):
            if ("kT", s) not in samp_state:
                samp_prepK(s)
            samp_state.pop(("kT", s))
            if ("v", s) not in samp_state:
                samp_load_v(s)
            vts = samp_state.pop(("v", s))
            tp, Bt = trb
            om, Bom = omr.get()
            scp = scpr.get(); scn = scpr.get()
            pts = [PTr.get(), PTr.get()]
            ptns = [PTnr.get(), PTnr.get()]
            for par in range(2):
                hb = par * 64
                psc, Bsc = scp[par]; psn, Bsn = scn[par]
                fns = []
                for kt in range(4):
                    for a in range(4):
                        fns.append(lambda e, kt=kt, a=a: e.matmul(psc[:, (kt * 4 + a) * TS:(kt * 4 + a + 1) * TS], lhsT=ckT[hb:hb + 64, a, kt * 128:(kt + 1) * 128],
                                                                  rhs=qnT[hb:hb + 64, a, s * TS:(s + 1) * TS], start=True, stop=True))
                for a in range(4):
                    fns.append(lambda e, a=a: e.matmul(psn[:, a * TS:(a + 1) * TS], lhsT=kT[hb:hb + 64, a, 0:128],
                                                       rhs=qnT[hb:hb + 64, a, s * TS:(s + 1) * TS], start=True, stop=True))
                k.grp(fns, reads=L_ckT + B_kT[0] + B_qnT, writes=[Bsc, Bsn])
            for par in range(2):
                psc, Bsc = scp[par]; psn, Bsn = scn[par]
                pt, Bpt = pts[par]; ptn, Bptn = ptns[par]
                k.op(act, lambda e: e.activation(out=pt[:], in_=psc[:], func=AF.Exp, scale=0.125), reads=[Bsc], writes=[Bpt])
                k.op(act, lambda e: e.activation(out=ptn[:], in_=psn[:, 0:128], func=AF.Exp, scale=0.125), reads=[Bsn], writes=[Bptn])
                eb5 = EB[:].rearrange("p (a b) m -> p a b m", b=2)
                for kt in range(4):
                    k.op(dve, lambda e, kt=kt: e.tensor_tensor(out=pt[:, kt * 128:(kt + 1) * 128].rearrange("p (h q) -> p h q", h=4),
                                                               in0=pt[:, kt * 128:(kt + 1) * 128].rearrange("p (h q) -> p h q", h=4),
                                                               in1=eb5[:, :, par, 512 - 128 * kt:512 - 128 * kt + TS], op=ALU.mult), reads=[Bpt, B_EB], writes=[Bpt])
                ebn5 = EBn[:].rearrange("p s (a b) q -> p s a b q", b=2)
                k.op(dve, lambda e: e.tensor_tensor(out=ptn[:].rearrange("p (h q) -> p h q", h=4), in0=ptn[:].rearrange("p (h q) -> p h q", h=4),
                                                    in1=ebn5[:, s, :, par, :], op=ALU.mult), reads=[Bptn, B_EB], writes=[Bptn])
                ppv, Bpv = pvb[par]
                fns = []
                for a in range(4):
                    h = 2 * a + par
                    for kt in range(4):
                        fns.append(lambda e, a=a, h=h, kt=kt: e.matmul(ppv[0:TS, a * 65:(a + 1) * 65], lhsT=pt[:, (kt * 4 + a) * TS:(kt * 4 + a + 1) * TS],
                                                                       rhs=vaug[:, vts[kt], h, :], start=(kt == 0), stop=False))
                    fns.append(lambda e, a=a, h=h: e.matmul(ppv[0:TS, a * 65:(a + 1) * 65], lhsT=ptn[:, a * TS:(a + 1) * TS], rhs=vaug[:, 0, h, :],
                                                            start=False, stop=True))
                k.grp(fns, reads=[Bpt, Bptn, B_v[0]] + [B_v[t_] for t_ in vts], writes=[Bpv])
            if s + 1 < NSEQ:
                samp_load_v(s + 1)
                samp_prepK(s + 1)
            for par in range(2):
                attn_epilogue(pvb[par][0], pvb[par][1], par, TS)
            k.op(dve, lambda e: e.tensor_tensor(out=om[0:TS, :], in0=o32[0:TS, :], in1=gass[:, s, :], op=ALU.mult), reads=[B_o32, B_gass[s]], writes=[Bom])
            k.grp([lambda e, c=c: e.transpose(out=tp[:, c * TS:(c + 1) * TS], in_=om[0:TS, c * 128:(c + 1) * 128], identity=ident[0:TS, 0:TS]) for c in range(4)],
                  reads=[Bom, B_c2], writes=[Bt])
            k.op(act, lambda e: e.activation(out=mixT[:, 4:8, s * TS:(s + 1) * TS], in_=tp[:, 0:4 * TS].rearrange("p (c t) -> p c t", c=4), func=AF.Copy),
                 reads=[Bt], writes=[B_mix[0]])

        def k_out(blk, jl):
            pt, Bp = mmr.get()
            k.grp([lambda e, c=c: e.transpose(out=pt[:, c * 128:(c + 1) * 128], in_=kn32l[c][0][:, jl * 128:(jl + 1) * 128], identity=identf[:]) for c in range(4)],
                  reads=B_kn32 + [B_c2], writes=[Bp])
            ko, Bko = koutr.get()
            k.op(dve, lambda e: e.tensor_copy(out=ko[:], in_=pt[:]), reads=[Bp], writes=[Bko])
            dst = pk[jl * 128:(jl + 1) * 128, :] if blk == 3 else sk
            k.dma(sp, dst, ko[:], Bko, reads=[Bko], is_out=True)

        phase2_state = {}

        back_pre = {}

        def back_load(tile):
            xt, Bx = xring.get()
            k.dma(sp, xt[:], x_src(tile), Bx, writes=[Bx])
            back_pre[tile] = (xt, Bx)

        def back_a(tile, jl):
            if tile not in back_pre:
                back_load(tile)
            xt, Bx = back_pre.pop(tile)
            pb, Bpb = pbr.get()
            k.dma(pq, pb[:], pp[tile * 128:(tile + 1) * 128, :] if tile < 16 else pss, Bpb, writes=[Bpb])
            for n in range(2):
                pt, Bp = mm2.get()
                k.grp([lambda e, kc=kc, n=n: e.matmul(pt[:], lhsT=mixT[:, kc, jl * 128:(jl + 1) * 128], rhs=w_out_sb[:, kc, n * 512:(n + 1) * 512],
                                                      start=(kc == 0), stop=(kc == 7)) for kc in range(8)], reads=[B_mix[jl], B_wout], writes=[Bp])
                k.op(dve, lambda e, n=n, pt=pt: e.tensor_tensor(out=xt[:, n * 512:(n + 1) * 512], in0=pt[:], in1=xt[:, n * 512:(n + 1) * 512], op=ALU.add),
                     reads=[Bp, Bx], writes=[Bx])
            hn, Bhn = hnr.get()
            k.op(act, lambda e: e.activation(out=hn[:], in_=xt[:], func=AF.Square, accum_out=pssq[:, tile:tile + 1]), reads=[Bx, B_c2], writes=[Bhn, B_pssq[tile]])
            k.op(act, lambda e: e.activation(out=prstd[:, tile:tile + 1], in_=pssq[:, tile:tile + 1], func=AF.Sqrt, scale=1.0 / D, bias=epst[:]),
                 reads=[B_pssq[tile]], writes=[B_pssq[tile]])
            k.op(dve, lambda e: e.reciprocal(out=prstd[:, tile:tile + 1], in_=prstd[:, tile:tile + 1]), reads=[B_pssq[tile]], writes=[B_pssq[tile]])
            k.op(dve, lambda e: e.scalar_tensor_tensor(out=hn[:], in0=xt[:], scalar=prstd[:, tile:tile + 1], in1=pg_bc[:], op0=ALU.mult, op1=ALU.mult),
                 reads=[Bx, B_pssq[tile], B_g2], writes=[Bhn])
            phase2_state[tile] = (xt, Bx, hn, Bhn, pb, Bpb)

        def back_a2(tile):
            xt, Bx, hn, Bhn, pb, Bpb = phase2_state.pop(tile)
            tp, Bt = trb
            k.grp([lambda e, c=c: e.transpose(out=tp[:, c * 128:(c + 1) * 128], in_=hn[:, c * 128:(c + 1) * 128], identity=ident[:]) for c in range(8)],
                  reads=[Bhn, B_c2], writes=[Bt])
            hnT, BhT = hnTr.get()
            k.op(act, lambda e: e.activation(out=hnT[:], in_=tp[:].rearrange("p (c t) -> p c t", c=8), func=AF.Copy), reads=[Bt], writes=[BhT])
            k.grp([lambda e, c=c: e.transpose(out=tp[:, c * 128:(c + 1) * 128], in_=pb[:, c * 128:(c + 1) * 128], identity=ident[:]) for c in range(2)],
                  reads=[Bpb, B_c2], writes=[Bt])
            pT, BpT = pTr.get()
            k.op(act, lambda e: e.activation(out=pT[:], in_=tp[:, 0:256].rearrange("p (c t) -> p c t", c=2), func=AF.Copy), reads=[Bt], writes=[BpT])
            phase2_state[tile] = (xt, Bx, hnT, BhT, pT, BpT)

        def back_b(tile):
            xt, Bx, hnT, BhT, pT, BpT = phase2_state.pop(tile)
            for n in range(2):
                pg_, Bg = mm2.get()
                k.grp([lambda e, kc=kc, n=n: e.matmul(pg_[:], lhsT=hnT[:, kc, :], rhs=w_pg_sb[:, kc, n * 512:(n + 1) * 512], start=(kc == 0), stop=(kc == 7))
                       for kc in range(8)], reads=[BhT, B_wpg], writes=[Bg])
                pe_, Be = mm2.get()
                k.grp([lambda e, kc=kc, n=n: e.matmul(pe_[:], lhsT=pT[:, kc, :], rhs=w_pe_sb[:, kc, n * 512:(n + 1) * 512], start=(kc == 0), stop=(kc == 1))
                       for kc in range(2)], reads=[BpT, B_wpe], writes=[Be])
                gt, Bgt = gater.get()
                k.op(act, lambda e: e.activation(out=gt[:], in_=pg_[:], func=AF.Sigmoid), reads=[Bg], writes=[Bgt])
                k.op(dve, lambda e: e.tensor_tensor(out=gt[:], in0=pe_[:], in1=gt[:], op=ALU.mult), reads=[Be, Bgt], writes=[Bgt])
                k.op(pool, lambda e, n=n: e.tensor_tensor(out=xt[:, n * 512:(n + 1) * 512], in0=xt[:, n * 512:(n + 1) * 512], in1=gt[:], op=ALU.add),
                     reads=[Bgt, Bx], writes=[Bx])
            dst = yp[tile * 128:(tile + 1) * 128, :] if tile < 16 else ys
            k.dma(sp, dst, xt[:], Bx, reads=[Bx], is_out=True)

        blocks = [(0, 512), (512, 512), (1024, 512), (1536, 512), (2048, 128)]
        for blk, (t0, N) in enumerate(blocks):
            if STAGE < 1 or (STAGE < 6 and blk > 0) or blk >= int(os.environ.get("MK_NBLK", "5")):
                break
            sample = blk == 4
            ntile = N // 128
            tile0 = t0 // 128
            front_a(tile0, 0)
            for jl in range(ntile):
                if jl + 1 < ntile:
                    front_a(tile0 + jl + 1, jl + 1)
                front_b(tile0 + jl, jl)
            if STAGE < 2:
                break
            prev = None
            for f in range(8, 16):
                cur = (f, qk1(f, N))
                if prev is not None:
                    qk2(prev[0], N, blk, prev[0] >= 12, prev[1])
                prev = cur
            qk2(prev[0], N, blk, True, prev[1])
            def vga(jl):
                vt = ((t0 // 128) + jl) % 8 if not sample else 0
                pt, Bp = mm_tm(jl, 2048, jl * 128, 128)
                k.op(act, lambda e, pt=pt, vt=vt: e.activation(out=vaug[:, vt, :, 0:64], in_=pt[:].rearrange("p (h d) -> p h d", h=8), func=AF.Copy),
                     reads=[Bp], writes=[B_v[vt]])
                if blk >= 3 and 'v' in FLG:
                    vo, Bvo = voutr.get()
                    k.op(dve, lambda e, pt=pt, vo=vo: e.tensor_copy(out=vo[:], in_=pt[:]), reads=[Bp, B_v[vt]], writes=[Bvo])
                    k.dma(sp, pv[jl * 128:(jl + 1) * 128, :] if blk == 3 else sv, vo[:], Bvo, reads=[Bvo], is_out=True)
                if not sample:
                    pt, Bp = mm_tm(jl, 2560, jl * 128, 128)
                    k.op(act, lambda e, pt=pt, jl=jl: e.activation(out=gas[:, jl, :], in_=pt[:], func=AF.Silu), reads=[Bp], writes=[B_gas[jl]])
                else:
                    for s in range(NSEQ):
                        pt, Bp = mm_tm(0, 2560, s * TS, TS)
                        k.op(act, lambda e, pt=pt, s=s: e.activation(out=gass[:, s, :], in_=pt[0:TS, :], func=AF.Silu), reads=[Bp], writes=[B_gass[s]])
            if blk >= 3 and 'k' in FLG:
                for jl in range(ntile):
                    k_out(blk, jl)

            def xlgl(c):
                if not sample and blk > 0:
                    k.op(pool, lambda e, c=c: e.tensor_copy(out=xl_ext[:, c, 0:3], in_=xl_ext[:, c, 512:515]), reads=[B_xl[c]], writes=[B_xl[c]])
                pt, Bp = mm_fm(c, N)
                if not sample:
                    k.op(act, lambda e, pt=pt, c=c: e.activation(out=xl_ext[:, c, 3:3 + N], in_=pt[:, 0:N], func=AF.Copy), reads=[Bp], writes=[B_xl[c]])
                    if blk == 3 and 'p' in FLG:
                        k.op(dve, lambda e, pt=pt, c=c: e.tensor_copy(out=pc32[:, c, :], in_=pt[:, N - 3:N]), reads=[Bp, B_xl[c]], writes=[B_pc])
                else:
                    k.op(act, lambda e, pt=pt, c=c: e.activation(out=xl_exs[:, c, :, 3:3 + TS], in_=pt[:, 0:N].rearrange("p (s t) -> p s t", s=NSEQ), func=AF.Copy),
                         reads=[Bp], writes=[B_xl[c]])
                    k.op(dve, lambda e, pt=pt, c=c: e.tensor_copy(out=sc32[:, c, :, :], in_=pt[:, 0:N].rearrange("p (s t) -> p s t", s=NSEQ)[:, :, TS - 3:TS]),
                         reads=[Bp, B_xl[c]], writes=[B_sc])
                pt, Bp = mm_fm(4 + c, N)
                k.op(act, lambda e, pt=pt, c=c: e.activation(out=gls[:, c, 0:N], in_=pt[:, 0:N], func=AF.Silu), reads=[Bp], writes=[B_gls[c]])
            if STAGE < 3:
                break
            xlgl(0); xlgl(1)
            if blk == 0:
                build_EB()
            ga_pair([0, 1], N, blk, to_gls=True)
            for jl in range(ntile):
                vga(jl)
            xlgl(2); xlgl(3)
            k.op(dve, lambda e: e.tensor_copy(out=mixT[:, 0:2, 0:N], in_=gls[:, 0:2, 0:N]), reads=[B_gls[0], B_gls[1]], writes=B_mix[0:ntile])
            ga_tail = ga_pair([2, 3], N, blk, defer=not sample)
            for jl in range(min(ntile, 3)):
                back_load(tile0 + jl)
            if not sample:
                attn_block(blk, ga_tail)
                if blk == 3:
                    samp_load_k(0)
                    samp_load_v(0)
            else:
                for s in range(NSEQ):
                    attn_sample(s)
            if STAGE < 5:
                break
            back_a(tile0, 0)
            for jl in range(ntile):
                if jl + 1 < ntile:
                    back_a(tile0 + jl + 1, jl + 1)
                back_a2(tile0 + jl)
                back_b(tile0 + jl)
                if jl + 3 < ntile:
                    back_load(tile0 + jl + 3)
                if ntile == 4 and blk + 1 < len(blocks) and jl in (1, 2):
                    nt = blocks[blk + 1][0] // 128 + (jl - 1)
                    if nt < blocks[blk + 1][0] // 128 + blocks[blk + 1][1] // 128:
                        preload[nt] = front_load(nt)

        if STAGE < 99:
            k.op(dve, lambda e: e.memset(pc32[:], 0.0), writes=[B_pc, B_ph, B_sc, B_sh])
        k.dma(sp, pc_t, pc32[:], B_pc, reads=[B_pc], is_out=True)
        k.dma(sp, ph_t, ph32[:], B_ph, reads=[B_ph], is_out=True)
        k.dma(sp, sc_t, sc32[:], B_sc, reads=[B_sc], is_out=True)
        k.dma(sp, sh_t, sh32[:], B_sh, reads=[B_sh], is_out=True)
        k.finish()
    return nc


_NC_CACHE = {}


def _prep_inputs(inp):
    f = lambda a: np.ascontiguousarray(a, dtype=np.float32)
    w_in = f(inp["w_in"][0]); w_out = f(inp["w_out"][0]); w_pg = f(inp["w_ple_gate"][0]); w_pe = f(inp["w_ple_proj"][0])
    ng = f(inp["norm_g"][0][None, :]); pg = f(inp["ple_norm_g"][0][None, :])
    cvs = np.stack([inp["conv_w"][0][0], inp["conv_w"][0][1], inp["conv_w"][0][2], inp["conv_w"][0][3], inp["conv_b"][0],
                    inp["gate_a_b"][0], inp["gate_x_b"][0], inp["lru_lambda"][0]])
    chanv = f(cvs.reshape(8, 4, 128).transpose(2, 0, 1))
    qkg = f(np.stack([np.tile(inp["q_norm_g"][0], 2), np.tile(inp["k_norm_g"][0], 2)], axis=1))

    def blockdiag(w):
        o = np.zeros((128, 4, 128), np.float32)
        for c in range(4):
            o[0:64, c, 0:64] = w[2 * c]
            o[64:128, c, 64:128] = w[2 * c + 1]
        return o
    wab = blockdiag(inp["gate_a_w"][0]); wxb = blockdiag(inp["gate_x_w"][0])
    relb = f(inp["rel_bias"][0].T)
    shared = dict(w_in=w_in, w_out=w_out, w_pg=w_pg, w_pe=w_pe, ng=ng, pg=pg, chanv=chanv, qkg=qkg, wab=wab, wxb=wxb, relb=relb)
    maps = []
    for c in range(NCORES):
        sl = slice(c * NSEQ, (c + 1) * NSEQ)
        m = dict(shared)
        m["xp"] = f(inp["x_prompt"][c]); m["xs"] = f(inp["x_sample"][sl].reshape(128, D))
        m["pp"] = f(inp["p_prompt"][0, c]); m["pss"] = f(inp["p_sample"][0, sl].reshape(128, 256))
        m["ck"] = f(inp["cache_k"][0, sl].reshape(NSEQ, 512, 512)); m["cv"] = f(inp["cache_v"][0, sl].reshape(NSEQ, 512, 512))
        m["sconv"] = f(inp["state_conv"][0, sl].reshape(NSEQ, 3, 4, 128).transpose(3, 2, 0, 1))
        m["slru"] = f(inp["state_lru"][0, sl].reshape(NSEQ, 4, 128).transpose(2, 1, 0))
        maps.append(m)
    return maps


def kernel(**inp):
    if "nc" not in _NC_CACHE:
        _NC_CACHE["nc"] = build_nc(int(os.environ.get("MK_STAGE", "99")))
    nc = _NC_CACHE["nc"]
    maps = _prep_inputs(inp)
    res = run_bass_kernel_spmd(nc, maps, core_ids=list(range(NCORES)))
    R = res.results
    yp = np.stack([R[c]["yp"] for c in range(NCORES)])
    ys = np.concatenate([R[c]["ys"].reshape(NSEQ, TS, D) for c in range(NCORES)])
    pk = np.stack([R[c]["pk"].reshape(512, 8, 64) for c in range(NCORES)])[None]
    pv = np.stack([R[c]["pv"].reshape(512, 8, 64) for c in range(NCORES)])[None]
    pc = np.stack([R[c]["pc_t"].transpose(2, 1, 0).reshape(3, 512) for c in range(NCORES)])[None]
    ph = np.stack([R[c]["ph_t"].transpose(1, 0).reshape(512) for c in range(NCORES)])[None]
    sk = np.concatenate([R[c]["sk"].reshape(NSEQ, TS, 8, 64) for c in range(NCORES)])[None]
    sv = np.concatenate([R[c]["sv"].reshape(NSEQ, TS, 8, 64) for c in range(NCORES)])[None]
    sc = np.concatenate([R[c]["sc_t"].transpose(2, 3, 1, 0).reshape(NSEQ, 3, 512) for c in range(NCORES)])[None]
    sh = np.concatenate([R[c]["sh_t"].transpose(2, 1, 0).reshape(NSEQ, 512) for c in range(NCORES)])[None]
    outs = (yp, ys, pk, pv, pc, ph, sk, sv, sc, sh)
    return tuple(np.ascontiguousarray(o, dtype=np.float32) for o in outs)
```
